# Optimizing a Trainium2 kernel written in Bass

```python
import math
import jax, jax.numpy as jnp
from jax import lax
import numpy as np

D_MODEL = 1024
BATCH = 8
SEQ = 2048
DEPTH = 1
DEC_BATCH = 32
DEC_SEQ = 64
PAST_LEN = 2048

CHUNK = 64
H_RET = 4
DK_HEAD = D_MODEL // H_RET
DV_HEAD = 2 * DK_HEAD
D_QK = H_RET * DK_HEAD
D_V = H_RET * DV_HEAD
D_CONV = D_MODEL
CONV_WIDTH = 3
D_FF = int(math.ceil(8 * D_MODEL / 3 / 256) * 256)
ROPE_BASE = 10000.0
LN_EPS = 1e-5
RMS_EPS = 1e-6
DEEPNORM_ALPHA = (2.0 * DEPTH) ** 0.25
DEEPNORM_BETA = (8.0 * DEPTH) ** -0.25
SPLIT_WIDTHS = (D_QK, D_QK, D_V, D_V, D_CONV, D_CONV, D_CONV, D_MODEL, D_MODEL)
SPLIT_IDX = tuple(int(v) for v in np.cumsum(SPLIT_WIDTHS)[:-1])
D_IN = int(sum(SPLIT_WIDTHS))

kernel_name = "hybrid_retention_shortconv_deepnorm_step"


def layer_norm(x, g, b):
    xf = x.astype(jnp.float32)
    mu = jnp.mean(xf, axis=-1, keepdims=True)
    xc = xf - mu
    var = jnp.mean(xc * xc, axis=-1, keepdims=True)
    return (xc * lax.rsqrt(var + LN_EPS) * g.astype(jnp.float32) + b.astype(jnp.float32)).astype(x.dtype)


def rotary(t, pos):
    half = DK_HEAD // 2
    inv = 1.0 / (ROPE_BASE ** jnp.linspace(0.0, 1.0, half, dtype=jnp.float32))
    ang = pos.astype(jnp.float32)[:, None] * inv[None, :]
    cos = jnp.cos(ang)[None, :, None, :]
    sin = jnp.sin(ang)[None, :, None, :]
    t1, t2 = t[..., :half], t[..., half:]
    return jnp.concatenate([t1 * cos - t2 * sin, t1 * sin + t2 * cos], axis=-1)


def retention(q, k, v, S0, chunk_len):
    B, L = q.shape[0], q.shape[1]
    n_chunks = L // chunk_len
    log_g = jnp.log(1.0 - 2.0 ** (-5.0 - jnp.arange(H_RET, dtype=jnp.float32)))
    idx = jnp.arange(chunk_len, dtype=jnp.float32)
    dist = jnp.abs(idx[:, None] - idx[None, :])
    intra_dec = jnp.exp(dist[None] * log_g[:, None, None])
    inter_dec = jnp.exp((idx[None, :] + 1.0) * log_g[:, None]).T
    upd_dec = jnp.exp((chunk_len - 1.0 - idx)[None, :] * log_g[:, None]).T
    carry_dec = jnp.exp(chunk_len * log_g)

    def to_chunks(t):
        return jnp.moveaxis(t.reshape((B, n_chunks, chunk_len) + t.shape[2:]), 1, 0)

    def step(S, qkv):
        qc, kc, vc = qkv
        s = jnp.einsum('bihd,bjhd->bhij', qc, kc) * intra_dec[None]
        o = jnp.einsum('bhij,bjhe->bihe', s, vc)
        o = o + jnp.einsum('bihd,bhde->bihe', qc, S) * inter_dec[None, :, :, None]
        S = S * carry_dec[None, :, None, None] + jnp.einsum(
            'bjhd,bjhe->bhde', kc * upd_dec[None, :, :, None], vc)
        return S, o

    S, o = lax.scan(step, S0, (to_chunks(q), to_chunks(k), to_chunks(v)))
    o = jnp.moveaxis(o, 0, 1).reshape(B, L, H_RET, DV_HEAD)
    return o, S


def hybrid_layer(x, pos, S0, buf0, chunk_len, w_in, w_conv, w_ret_out, w_conv_out, w_o,
                 ln1_g, ln1_b, w_up, w_down, ln2_g, ln2_b):
    B, L, _ = x.shape
    proj = x @ w_in
    q, k, v, g, cb, cc, cx, gr, gc = jnp.split(proj, SPLIT_IDX, axis=-1)
    q = rotary(q.reshape(B, L, H_RET, DK_HEAD).astype(jnp.float32), pos)
    k = rotary(k.reshape(B, L, H_RET, DK_HEAD).astype(jnp.float32), pos) * (DK_HEAD ** -0.5)
    v = v.reshape(B, L, H_RET, DV_HEAD).astype(jnp.float32)
    o, S = retention(q, k, v, S0.astype(jnp.float32), chunk_len)
    o = o * lax.rsqrt(jnp.mean(o * o, axis=-1, keepdims=True) + RMS_EPS)
    o = o.reshape(B, L, D_V)
    r = (jax.nn.silu(g.astype(jnp.float32)) * o).astype(x.dtype) @ w_ret_out
    u = cc * cx
    padded = jnp.concatenate([buf0.astype(u.dtype), u], axis=1)
    conv = (w_conv[0] * padded[:, 0:L] + w_conv[1] * padded[:, 1:L + 1]
            + w_conv[2] * padded[:, 2:L + 2])
    c = (cb * conv) @ w_conv_out
    m = jax.nn.sigmoid(gr) * r + jax.nn.sigmoid(gc) * c
    h = layer_norm(DEEPNORM_ALPHA * x + m @ w_o, ln1_g, ln1_b)
    a, b = jnp.split(h @ w_up, 2, axis=-1)
    y = layer_norm(DEEPNORM_ALPHA * h + (jax.nn.silu(a) * b) @ w_down, ln2_g, ln2_b)
    return y, S, padded[:, L:]


def setup_inputs(seed: int = 0) -> dict:
    key = jax.random.key(seed)
    ks = jax.random.split(key, 16)
    f32 = jnp.float32
    nrm = lambda k, shape: jax.random.normal(k, shape, f32)
    return {
        "x_prompt": nrm(ks[0], (BATCH, SEQ, D_MODEL)),
        "x_sample": nrm(ks[1], (DEC_BATCH, DEC_SEQ, D_MODEL)),
        "state_ret": 0.1 * nrm(ks[2], (DEPTH, DEC_BATCH, H_RET, DK_HEAD, DV_HEAD)),
        "cache_conv": nrm(ks[3], (DEPTH, DEC_BATCH, CONV_WIDTH - 1, D_CONV)),
        "w_in": nrm(ks[4], (DEPTH, D_MODEL, D_IN)) * D_MODEL ** -0.5,
        "w_conv": nrm(ks[5], (DEPTH, CONV_WIDTH, D_CONV)) * CONV_WIDTH ** -0.5,
        "w_ret_out": nrm(ks[6], (DEPTH, D_V, D_MODEL)) * (DEEPNORM_BETA * D_V ** -0.5),
        "w_conv_out": nrm(ks[7], (DEPTH, D_CONV, D_MODEL)) * (DEEPNORM_BETA * D_CONV ** -0.5),
        "w_o": nrm(ks[8], (DEPTH, D_MODEL, D_MODEL)) * (DEEPNORM_BETA * D_MODEL ** -0.5),
        "ln1_g": 1.0 + 0.02 * nrm(ks[9], (DEPTH, D_MODEL)),
        "ln1_b": 0.02 * nrm(ks[10], (DEPTH, D_MODEL)),
        "w_up": nrm(ks[11], (DEPTH, D_MODEL, 2 * D_FF)) * (DEEPNORM_BETA * D_MODEL ** -0.5),
        "w_down": nrm(ks[12], (DEPTH, D_FF, D_MODEL)) * (DEEPNORM_BETA * D_FF ** -0.5),
        "ln2_g": 1.0 + 0.02 * nrm(ks[13], (DEPTH, D_MODEL)),
        "ln2_b": 0.02 * nrm(ks[14], (DEPTH, D_MODEL)),
    }


def reference(x_prompt, x_sample, state_ret, cache_conv, w_in, w_conv, w_ret_out, w_conv_out,
              w_o, ln1_g, ln1_b, w_up, w_down, ln2_g, ln2_b):
    bp, lp = x_prompt.shape[0], x_prompt.shape[1]
    ls = x_sample.shape[1]
    pos_p = jnp.arange(lp, dtype=jnp.int32)
    pos_s = PAST_LEN + jnp.arange(ls, dtype=jnp.int32)
    S_zero = jnp.zeros((bp, H_RET, DK_HEAD, DV_HEAD), jnp.float32)
    buf_zero = jnp.zeros((bp, CONV_WIDTH - 1, D_CONV), x_prompt.dtype)
    xp, xs = x_prompt, x_sample
    sp_list, bp_list, ss_list, bs_list = [], [], [], []
    for l in range(DEPTH):
        params = (w_in[l], w_conv[l], w_ret_out[l], w_conv_out[l], w_o[l],
                  ln1_g[l], ln1_b[l], w_up[l], w_down[l], ln2_g[l], ln2_b[l])
        xp, sp, bufp = hybrid_layer(xp, pos_p, S_zero, buf_zero, CHUNK, *params)
        xs, ss, bufs = hybrid_layer(xs, pos_s, state_ret[l], cache_conv[l], ls, *params)
        sp_list.append(sp.astype(state_ret.dtype))
        bp_list.append(bufp.astype(cache_conv.dtype))
        ss_list.append(ss.astype(state_ret.dtype))
        bs_list.append(bufs.astype(cache_conv.dtype))
    return (xp, xs, jnp.stack(sp_list), jnp.stack(bp_list), jnp.stack(ss_list), jnp.stack(bs_list))
```

```python
import math
from contextlib import ExitStack

import numpy as np
import concourse.bass as bass
import concourse.mybir as mybir
from concourse.bass_utils import run_bass_kernel_spmd

F32 = mybir.dt.float32
BF16 = mybir.dt.bfloat16
AF = mybir.ActivationFunctionType
ALU = mybir.AluOpType

D = 1024
SEQ = 2048
DEC_SEQ = 64
NSAMP = 4
H = 4
DK = 256
DV = 512
D_IN = 11264
D_FF = 2816
NJ = D_FF // 128
LN_EPS = 1e-5
RMS_EPS = 1e-6
ALPHA = 2.0 ** 0.25
PAST = 2048
NCORES = 8

OQ, OK_, OV, OG, OCB, OCC, OCX, OGR, OGC = 0, 1024, 2048, 4096, 6144, 7168, 8192, 9216, 10240


class Sched:
    def __init__(self, nc, es):
        self.nc = nc
        self.es = es
        self.E = {"pe": nc.tensor, "act": nc.scalar, "dve": nc.vector, "pool": nc.gpsimd, "sp": nc.sync}
        self.sem = {k: es.enter_context(nc.semaphore("sem_" + k)) for k in ("pe", "act", "dve", "pool")}
        self.cnt = {k: 0 for k in self.sem}
        self.dsem = {}
        self.dcnt = {}
        self.seen = {k: {} for k in self.E}
        self.lw = {}
        self.rd = {}

    def _wait(self, eng, tok):
        name, sem, val = tok
        if self.seen[eng].get(name, 0) >= val:
            return
        self.E[eng].wait_ge(sem, val)
        self.seen[eng][name] = val

    def _sync(self, eng, reads, writes):
        toks = []
        for r in reads:
            toks.extend(self.lw.get(r, {}).values())
        for w in writes:
            for t in self.lw.get(w, {}).values():
                if t[0] != eng or (eng != "pe" and w in ("junk",)):
                    toks.append(t)
            for t in self.rd.get(w, {}).values():
                if t[0] != eng or eng != "pe":
                    toks.append(t)
        for t in toks:
            self._wait(eng, t)

    def _record(self, tok, reads, writes):
        for r in reads:
            self.rd.setdefault(r, {})[tok[0]] = tok
        for w in writes:
            self.lw.setdefault(w, {})[tok[0]] = tok
            self.rd[w] = {}

    def op(self, eng, fn, reads=(), writes=()):
        self._sync(eng, reads, writes)
        inst = fn()
        self.cnt[eng] += 1
        inst.then_inc(self.sem[eng], 1)
        tok = (eng, self.sem[eng], self.cnt[eng])
        self._record(tok, reads, writes)
        return tok

    def dma(self, q, key, pairs, reads=(), writes=()):
        if key not in self.dsem:
            self.dsem[key] = self.es.enter_context(self.nc.semaphore("dsem_" + key))
            self.dcnt[key] = 0
        self._sync(q, reads, writes)
        for (o, i) in pairs:
            self.E[q].dma_start(out=o, in_=i).then_inc(self.dsem[key], 16)
            self.dcnt[key] += 16
        tok = ("d_" + key, self.dsem[key], self.dcnt[key])
        self._record(tok, reads, writes)
        return tok

    def barrier(self):
        for e in self.E:
            for k in self.sem:
                if k != e and self.cnt[k] > 0:
                    self._wait(e, (k, self.sem[k], self.cnt[k]))
            for k in self.dsem:
                if self.dcnt[k] > 0:
                    self._wait(e, ("d_" + k, self.dsem[k], self.dcnt[k]))

    def finish(self):
        for k in self.sem:
            if self.cnt[k] > 0:
                self._wait("sp", (k, self.sem[k], self.cnt[k]))
        for k in self.dsem:
            if self.dcnt[k] > 0:
                self._wait("sp", ("d_" + k, self.dsem[k], self.dcnt[k]))


def _constants():
    f32 = np.float32
    half = DK // 2
    inv = 1.0 / (10000.0 ** np.linspace(0.0, 1.0, half, dtype=np.float64))
    pos = np.concatenate([np.tile(PAST + np.arange(DEC_SEQ), NSAMP), np.arange(SEQ)]).astype(np.float64)
    ang = pos[None, :] * inv[:, None]
    cs = np.stack([np.cos(ang), np.sin(ang)], axis=1).astype(f32)

    lg = np.log(1.0 - 2.0 ** (-5.0 - np.arange(H, dtype=np.float64)))
    i = np.arange(128)
    decq = np.zeros((3, H, 128), np.float64)
    updk = np.zeros((128, 3, H), np.float64)
    for h in range(H):
        decq[0, h] = np.exp((i + 1) * lg[h])
        decq[1, h, :64] = np.exp((i[:64] + 1) * lg[h])
        decq[2, h, 64:] = np.exp((i[64:] - 64 + 1) * lg[h])
        updk[:, 0, h] = np.exp((127 - i) * lg[h]) / 16.0
        updk[:64, 1, h] = np.exp((63 - i[:64]) * lg[h]) / 16.0
        updk[64:, 2, h] = np.exp((127 - i[64:]) * lg[h]) / 16.0
    maskT = np.zeros((128, 2, H, 128), np.float64)
    jj, ii = np.meshgrid(i, i, indexing="ij")
    same = (jj // 64) == (ii // 64)
    for h in range(H):
        dec_p = np.exp((ii + 1) * lg[h])
        dec_s = np.exp(((ii % 64) + 1) * lg[h])
        intra = np.exp(np.abs(ii - jj) * lg[h])
        cross = np.exp((ii - jj) * lg[h])
        mp = np.where(same, intra, np.where(ii > jj, cross, 0.0))
        maskT[:, 0, h, :] = mp / dec_p / 16.0
        maskT[:, 1, h, :] = np.where(same, intra, 0.0) / dec_s / 16.0
    decq_b = np.broadcast_to(decq.reshape(1, 3 * H * 128), (128, 3 * H * 128))
    cdec = np.exp(np.array([128.0, 64.0])[:, None] * lg[None, :])
    return dict(
        cs=np.ascontiguousarray(cs),
        decq=np.ascontiguousarray(decq_b, dtype=f32),
        updk=np.ascontiguousarray(updk.reshape(128, 3 * H), dtype=f32),
        maskT=np.ascontiguousarray(maskT.reshape(128, 2 * H * 128), dtype=f32),
        ident=np.eye(128, dtype=f32),
    ), cdec


def build_program(cdec, stop_after=None):
    nc = bass.Bass("TRN2", target_bir_lowering=False)

    def din(name, shape):
        return nc.dram_tensor(name, list(shape), F32, kind="ExternalInput").ap()

    def dout(name, shape):
        return nc.dram_tensor(name, list(shape), F32, kind="ExternalOutput").ap()

    xp = din("xp", (SEQ, D))
    xsm = din("xs", (NSAMP * DEC_SEQ, D))
    sret = din("sret", (NSAMP, H, DK, DV))
    cconv = din("cconv", (NSAMP, 2, D))
    w_in = din("w_in", (D, D_IN))
    w_ro = din("w_ro", (2 * D, D))
    w_co = din("w_co", (D, D))
    w_o = din("w_o", (D, D))
    w_up = din("w_up", (D, 2 * D_FF))
    w_dn = din("w_dn", (D_FF, D))
    wc_d = din("wc", (128, 40))
    lnp_d = din("lnp", (4, 128, D))
    cs_d = din("cs", (128, 2, NSAMP * DEC_SEQ + SEQ))
    decq_d = din("decq", (128, 3 * H * 128))
    updk_d = din("updk", (128, 3 * H))
    mask_d = din("maskT", (128, 2 * H * 128))
    ident_d = din("ident", (128, 128))

    yp = dout("yp", (SEQ, D))
    ys = dout("ys", (NSAMP * DEC_SEQ, D))
    sp_o = dout("sp_o", (H, DK, DV))
    cp_o = dout("cp_o", (2, D))
    ss_o = dout("ss_o", (NSAMP, H, DK, DV))
    cs_o = dout("cs_o", (NSAMP, 2, D))

    w_in_v = w_in.rearrange("(kc p) n -> p kc n", p=128)
    w_ro_v = w_ro.rearrange("(kc p) n -> p kc n", p=128)
    w_co_v = w_co.rearrange("(kc p) n -> p kc n", p=128)
    w_o_v = w_o.rearrange("(kc p) n -> p kc n", p=128)
    w_up_v = w_up.rearrange("(kc p) n -> p kc n", p=128)
    w_dn_v = w_dn.rearrange("(kc p) n -> p kc n", p=128)

    TS = 768
    NTL = TS // 128
    es = ExitStack()
    with es:
        def sb(name, shape, dt):
            return es.enter_context(nc.sbuf_tensor("sb_" + name, list(shape), dt))

        R1N = 8 * TS + 2 * 1024 + 2 * TS + 8 * 256 + NTL * 512 * 2
        R1 = sb("R1", (128, R1N), BF16)
        R2 = sb("R2", (128, NTL * 1024), F32)
        R3 = sb("R3", (128, 8 * TS), BF16)
        St = sb("St", (128, 4, 2, 512), F32)
        Ss = sb("Ss", (128, 4, 2, 512), F32)
        Sbf = sb("Sbf", (128, 2, 2, 512), BF16)
        cst = sb("cst", (128, 2, TS), F32)
        xsb = [sb("xs0", (128, 1024), F32), sb("xs1", (128, 1024), F32)]
        NWB = 4
        Wb = [sb(f"W{i}", (128, 6144), BF16) for i in range(NWB)]
        lnp = [sb("lnp0", (128, 1024), F32), sb("lnp1", (128, 1024), F32)]
        UBN = 780
        ub = [sb("ub0", (128, UBN), F32), sb("ub1", (128, UBN), F32)]
        scrT = sb("scr", (128, 2048), F32)
        scr = [scrT[:, i * 512:(i + 1) * 512] for i in range(4)]
        maskt = sb("maskt", (128, 2, H, 128), F32)
        decq = sb("decq", (128, 3, H, 128), F32)
        updk = sb("updk", (128, 3, H), F32)
        wcs = sb("wcs", (128, 5, 8), F32)
        idf = sb("idf", (128, 128), F32)
        idb = sb("idb", (128, 128), BF16)
        ucar = sb("ucar", (128, 8, 2), F32)
        haloT = sb("haloT", (128, 8, 8), F32)
        scT = sb("scT", (128, 8, 8), F32)
        sTm = [sb("sTm0", (128, 128), BF16), sb("sTm1", (128, 128), BF16)]
        ogb = [sb("og0", (128, 512), BF16), sb("og1", (128, 512), BF16)]
        junk = sb("junk", (128, 512), BF16)
        stat = sb("stat", (128, 64), F32)
        mhalf = sb("mhalf", (128, 1), F32)
        P = [es.enter_context(nc.psum_tensor(f"P{i}", [128, 512], F32)) for i in range(8)]

        S = Sched(nc, es)

        def v3(t, off, a, b):
            return t[:, off:off + a * b].rearrange("p (a b) -> p a b", a=a)

        o_ = 0
        xT = v3(R1, o_, 8, TS); o_ += 8 * TS
        HB0 = o_
        qpT = v3(R1, o_, 2, 1024); o_ += 2048
        kT = v3(R1, o_, 2, TS); o_ += 2 * TS
        kp = v3(R1, o_, 8, 256); o_ += 2048
        vv = v3(R1, o_, NTL, 512); o_ += NTL * 512
        gg = v3(R1, o_, NTL, 512); o_ += NTL * 512
        assert o_ == R1N and R1N >= 22 * TS and R1N - HB0 >= 8 * TS
        mT = v3(R1, HB0, 8, TS)
        zT = v3(R1, 0, 22, TS)
        hh = v3(R2, 0, NTL, 1024)
        goT = R2[:].bitcast(BF16)[:, 0:16 * TS].rearrange("p (a b) -> p a b", a=16)
        cvT = v3(R3, 0, 8, TS)
        hT = v3(R3, 0, 8, TS)

        def wblk(wb, j):
            return wb[:, j * 2048:(j + 1) * 2048].rearrange("p (k n) -> p k n", k=8)

        S.dma("sp", "misc", [(maskt[:].rearrange("p a h i -> p (a h i)"), mask_d[:, :]),
                             (decq[:].rearrange("p a h i -> p (a h i)"), decq_d[:, :]),
                             (updk[:].rearrange("p a h -> p (a h)"), updk_d[:, :]),
                             (wcs[:].rearrange("p a c -> p (a c)"), wc_d[:, :]),
                             (idf[:], ident_d[:, :])],
              writes=["maskt", "decq", "updk", "wcs", "idf"])
        S.dma("pool", "misc2", [(idb[:], ident_d[:, :])], writes=["idb"])
        S.op("pool", lambda: nc.gpsimd.memset(mhalf[:], -0.5), writes=["mhalf"])

        hin = xsb[1][0:8, :]
        S.op("dve", lambda: nc.vector.memset(xsb[1][:], 0.0), writes=["xs1"])
        S.dma("sp", "xs1", [(hin, cconv.rearrange("s r d -> (s r) d"))], writes=["xs1"])

        for hf in range(2):
            def f_h(hf=hf):
                for k4 in range(4):
                    c = hf * 4 + k4
                    ins = nc.tensor.transpose(P[hf][:, k4 * 128:(k4 + 1) * 128], xsb[1][:, c * 128:(c + 1) * 128], idf[:])
                return ins
            S.op("pe", f_h, reads=["xs1", "idf"], writes=[f"P{hf}"])
            S.op("dve", lambda hf=hf: nc.vector.tensor_copy(out=haloT[:, hf * 4:(hf + 1) * 4, :],
                                                            in_=P[hf][:].rearrange("p (a b) -> p a b", a=4)[:, :, 0:8]),
                 reads=[f"P{hf}"], writes=["haloT"])

        def flush_rows(src_fn, nrows, dst):
            for hf in range(2):
                stg = scr[hf][:, 0:512].rearrange("p (a b) -> p a b", a=4)
                S.op("dve", lambda hf=hf, stg=stg: nc.vector.tensor_copy(out=stg[:, :, 0:nrows], in_=src_fn(hf * 4)),
                     reads=["ucar", "scT"], writes=[f"scr{hf}"])

                def f(hf=hf):
                    for k4 in range(4):
                        ins = nc.tensor.transpose(P[6 + hf][:, k4 * 128:(k4 + 1) * 128], scr[hf][:, k4 * 128:(k4 + 1) * 128], idf[:])
                    return ins
                S.op("pe", f, reads=[f"scr{hf}", "idf"], writes=[f"P{6 + hf}"])
                S.op("act", lambda hf=hf: nc.scalar.activation(out=hin[0:nrows, hf * 512:(hf + 1) * 512], in_=P[6 + hf][0:nrows, :], func=AF.Copy),
                     reads=[f"P{6 + hf}"], writes=["xs1"])
            S.dma("sp", "xs1", [(dst, hin[0:nrows, :])], reads=["xs1"])

        steps = []

        def add(loads, fn):
            steps.append((loads, fn))

        def emit_st(tiles, chunks, segs, col0, first, last):
            nt = len(tiles)
            T = nt * 128
            has_prompt = any(tl["kind"] == 0 for tl in tiles)
            has_sample = any(tl["kind"] == 1 for tl in tiles)
            for t, tl in enumerate(tiles):
                if tl["kind"] == 0:
                    tl["vars"] = [(0, t * 128, t, 0)]
                else:
                    tl["vars"] = [(1, t * 128, t, 0), (2, TS + t * 128, NTL + t, 1)]
            def xTn(tok0, ntok):
                return [f"xT{t_}" for t_ in range(tok0 // 128, (tok0 + ntok) // 128)]
            fpt = [t for t, tl in enumerate(tiles) if tl["kind"] == 0]
            first_prompt_tile = fpt[0] if fpt else None
            last_prompt_tile = fpt[-1] if fpt else None

            def ph_A0(_):
                S.dma("sp", "cs", [(cst[:, :, 0:T], cs_d[:, :, col0:col0 + T])], writes=["cst"])
                xslots = [(xsb[0][:], ["xs0"], "xs0"), (xsb[1][:], ["xs1"], "xs1"),
                          (scrT[:, 0:1024], ["scr0", "scr1"], "xq0"), (scrT[:, 1024:2048], ["scr2", "scr3"], "xq1")]
                for t, tl in enumerate(tiles):
                    xbuf, xnames, xkey = xslots[t % 4]
                    S.dma("sp", xkey, [(xbuf, tl["x"])], writes=xnames)
                    for hf in range(2):
                        bk = (2 * t + hf) % 8

                        def f(bk=bk, hf=hf, xbuf=xbuf):
                            for k4 in range(4):
                                kc = hf * 4 + k4
                                ins = nc.tensor.transpose(P[bk][:, k4 * 128:(k4 + 1) * 128],
                                                          xbuf[:, kc * 128:(kc + 1) * 128], idf[:])
                            return ins
                        S.op("pe", f, reads=xnames + ["idf"], writes=[f"P{bk}"])
                        src = P[bk][:].rearrange("p (a b) -> p a b", a=4)
                        dst = xT[:, hf * 4:hf * 4 + 4, t * 128:(t + 1) * 128]
                        S.op("act", lambda src=src, dst=dst: nc.scalar.activation(out=dst, in_=src, func=AF.Copy),
                             reads=[f"P{bk}"], writes=[f"xT{t}"])
            add(None, ph_A0)

            deferred = []

            def run_deferred():
                while deferred:
                    deferred.pop(0)()

            for h in range(H):
                def ph_QK(wb, h=h):
                    if has_sample:
                        seqs = sorted(s_ for tl in tiles if tl["kind"] == 1 for s_ in tl["seqs"])
                        S.dma("sp", "Sld", [(Ss[:, s_, :, :], sret[s_, h].rearrange("(two p) e -> p two e", p=128))
                                            for s_ in seqs], writes=["Ss"] + [f"Ss{s_}_{hf_}" for s_ in seqs for hf_ in range(2)])
                    if has_prompt and not first:
                        fs_ = first_prompt_tile % 2
                        S.op("act", lambda: nc.scalar.activation(out=Sbf[:, fs_, :, :], in_=St[:, h, :, :], func=AF.Copy),
                             reads=[f"St{h}_0", f"St{h}_1"], writes=[f"Sbf{fs_}"])
                    bi = 0
                    for (tok0, ntok, sg) in chunks:
                        tk = slice(tok0, tok0 + ntok)
                        ckind = segs[sg]["kind"]
                        for which in range(2):
                            b1, b2 = (0, 1) if bi % 2 == 0 else (2, 3)
                            bi += 1
                            for hf, bk in ((0, b1), (1, b2)):
                                def f(bk=bk, hf=hf, which=which, tk=tk, ntok=ntok):
                                    for kc in range(8):
                                        ins = nc.tensor.matmul(P[bk][:, 0:ntok],
                                                               wblk(wb, which)[:, kc, hf * 128: hf * 128 + 128],
                                                               xT[:, kc, tk], start=(kc == 0), stop=(kc == 7))
                                    return ins
                                S.op("pe", f, reads=["W"] + xTn(tok0, ntok), writes=[f"P{bk}"])
                            cos = cst[:, 0, tk]
                            sin = cst[:, 1, tk]
                            ra, rb, ra2, rb2 = (scr[i][:, 0:ntok] for i in range(4))
                            t1, t2 = P[b1][:, 0:ntok], P[b2][:, 0:ntok]
                            if which == 1:
                                o1, o2 = kT[:, 0, tk], kT[:, 1, tk]
                                w1, w2 = ["kT"], ["kT"]
                            else:
                                o1, o2 = ub[0][:, 0:ntok], ub[1][:, 0:ntok]
                                w1, w2 = ["ub0"], ["ub1"]
                            S.op("dve", lambda: nc.vector.tensor_tensor(out=ra, in0=t1, in1=cos, op=ALU.mult),
                                 reads=[f"P{b1}", "cst"], writes=["scr0"])
                            S.op("dve", lambda: nc.vector.tensor_tensor(out=rb, in0=t2, in1=sin, op=ALU.mult),
                                 reads=[f"P{b2}", "cst"], writes=["scr1"])
                            S.op("dve", lambda: nc.vector.tensor_tensor(out=ra2, in0=t1, in1=sin, op=ALU.mult),
                                 reads=[f"P{b1}", "cst"], writes=["scr2"])
                            S.op("dve", lambda: nc.vector.tensor_tensor(out=rb2, in0=t2, in1=cos, op=ALU.mult),
                                 reads=[f"P{b2}", "cst"], writes=["scr3"])
                            S.op("dve", lambda: nc.vector.tensor_tensor(out=o1, in0=ra, in1=rb, op=ALU.subtract),
                                 reads=["scr0", "scr1"], writes=w1)
                            S.op("dve", lambda: nc.vector.tensor_tensor(out=o2, in0=ra2, in1=rb2, op=ALU.add),
                                 reads=["scr2", "scr3"], writes=w2)
                            run_deferred()
                            if which == 0:
                                nct = ntok // 128
                                t0i = tok0 // 128
                                vlist = [(0, tok0)] if ckind == 0 else [(1, tok0), (2, TS + tok0)]
                                for (var, qcol) in vlist:
                                    for hf in range(2):
                                        src = ub[hf][:, 0:ntok].rearrange("p (a b) -> p a b", b=128)
                                        dq = decq[:, var, h, :].unsqueeze(1).broadcast_to([128, nct, 128])
                                        dst = qpT[:, hf, qcol:qcol + ntok].rearrange("p (a b) -> p a b", b=128)
                                        S.op("dve", lambda src=src, dq=dq, dst=dst: nc.vector.tensor_tensor(out=dst, in0=src, in1=dq, op=ALU.mult),
                                             reads=[f"ub{hf}", "decq"], writes=["qpT"])
                add([(0, 8, 256, w_in_v[:, :, OQ + h * 256: OQ + (h + 1) * 256]),
                     (2048, 8, 256, w_in_v[:, :, OK_ + h * 256: OK_ + (h + 1) * 256])], ph_QK)

                def ph_V(wb, h=h):
                    w = wb[:, 0:4096].rearrange("p (k n) -> p k n", k=8)
                    for t in range(nt):
                        bk = 1 + (t % 2)

                        def f(t=t, bk=bk):
                            for kc in range(8):
                                ins = nc.tensor.matmul(P[bk][:, :], xT[:, kc, t * 128:(t + 1) * 128], w[:, kc, :],
                                                       start=(kc == 0), stop=(kc == 7))
                            return ins
                        S.op("pe", f, reads=["W", f"xT{t}"], writes=[f"P{bk}"])
                        S.op("act", lambda t=t, bk=bk: nc.scalar.activation(out=vv[:, t, :], in_=P[bk][:, :], func=AF.Copy),
                             reads=[f"P{bk}"], writes=["vv"])
                add([(0, 8, 512, w_in_v[:, :, OV + h * 512: OV + (h + 1) * 512])], ph_V)

                def ph_G(wb, h=h):
                    w = wb[:, 0:4096].rearrange("p (k n) -> p k n", k=8)
                    for t in range(nt):
                        bk = 1 + (t % 2)

                        def f(t=t, bk=bk):
                            for kc in range(8):
                                ins = nc.tensor.matmul(P[bk][:, :], xT[:, kc, t * 128:(t + 1) * 128], w[:, kc, :],
                                                       start=(kc == 0), stop=(kc == 7))
                            return ins
                        S.op("pe", f, reads=["W", f"xT{t}"], writes=[f"P{bk}"])
                        S.op("act", lambda t=t, bk=bk: nc.scalar.activation(out=gg[:, t, :], in_=P[bk][:, :], func=AF.Silu),
                             reads=[f"P{bk}"], writes=["gg"])
                        tl = tiles[t]
                        bk = 4 + (t % 2)
                        pb = P[bk][:].bitcast(BF16)

                        def f(t=t, pb=pb):
                            for hf in range(2):
                                ins = nc.tensor.transpose(pb[:, hf * 128:(hf + 1) * 128], kT[:, hf, t * 128:(t + 1) * 128], idb[:])
                            return ins
                        S.op("pe", f, reads=["kT", "idb"], writes=[f"P{bk}"])
                        for (var, _q, kslot, _vi) in tl["vars"]:
                            S.op("act", lambda var=var, kslot=kslot, pb=pb: nc.scalar.activation(
                                out=kp[:, kslot, :], in_=pb[:, 0:256], func=AF.Identity, scale=updk[:, var, h:h + 1]),
                                reads=[f"P{bk}", "updk"], writes=["kp"])
                    def post_og(t):
                        sl = t % 2
                        bo = 1 + sl
                        col = (h * 8 + t) % 32
                        S.op("dve", lambda: nc.vector.scalar_tensor_tensor(
                            out=ogb[sl][:], in0=P[bo][:, :], scalar=stat[:, 32 + col:33 + col], in1=gg[:, t, :],
                            op0=ALU.mult, op1=ALU.mult),
                            reads=[f"P{bo}", f"rs{col}", "gg"], writes=[f"og{sl}"])

                    def post_tr(t):
                        sl = t % 2
                        tt = slice(t * 128, (t + 1) * 128)
                        p7 = P[7][:].bitcast(BF16)[:, 0:512]

                        def f_t():
                            for ec in range(4):
                                ins = nc.tensor.transpose(p7[:, ec * 128:(ec + 1) * 128], ogb[sl][:, ec * 128:(ec + 1) * 128], idb[:])
                            return ins
                        S.op("pe", f_t, reads=[f"og{sl}", "idb"], writes=["P7"])
                        S.op("act", lambda: nc.scalar.activation(
                            out=goT[:, h * 4:(h + 1) * 4, tt], in_=p7.rearrange("p (a b) -> p a b", a=4), func=AF.Copy),
                            reads=["P7"], writes=["goT"])

                    def scores(t):
                        tl = tiles[t]
                        tt = slice(t * 128, (t + 1) * 128)
                        sl = t % 2
                        variants = tl["vars"]

                        def f_s():
                            n_mm = 2 * len(variants)
                            c = 0
                            for (var, qcol, _k, _vi) in variants:
                                for hf in range(2):
                                    ins = nc.tensor.matmul(P[0][:, 0:128], kT[:, hf, tt], qpT[:, hf, qcol:qcol + 128],
                                                           start=(c == 0), stop=(c == n_mm - 1))
                                    c += 1
                            return ins
                        S.op("pe", f_s, reads=["kT", "qpT"], writes=["P0"])
                        S.op("dve", lambda: nc.vector.tensor_tensor(out=sTm[sl][:], in0=P[0][:, 0:128], in1=maskt[:, tl["kind"], h, :], op=ALU.mult),
                             reads=["P0", "maskt"], writes=[f"sTm{sl}"])

                    scores(0)
                    for t, tl in enumerate(tiles):
                        kind = tl["kind"]
                        variants = tl["vars"]
                        zero_state = (kind == 0 and first and t == first_prompt_tile)
                        sl = t % 2
                        if t + 1 < nt:
                            scores(t + 1)
                        if kind == 1:
                            for vi in range(2):
                                S.op("act", lambda vi=vi, tl=tl: nc.scalar.activation(out=Sbf[:, vi, :, :], in_=Ss[:, tl["seqs"][vi], :, :], func=AF.Copy),
                                     reads=[f"Ss{tl['seqs'][vi]}_0", f"Ss{tl['seqs'][vi]}_1"], writes=[f"Sbf{vi}"])
                        if t > 1:
                            post_og(t - 2)
                        upd = []
                        for (var, _q, kslot, vi) in variants:
                            for hf in range(2):
                                if kind == 0:
                                    bk = 3 + 2 * sl + hf
                                    sdst = St[:, h, hf, :]
                                    sres = f"St{h}_{hf}"
                                else:
                                    bk = 3 + 2 * vi + hf
                                    sdst = Ss[:, tl["seqs"][vi], hf, :]
                                    sres = f"Ss{tl['seqs'][vi]}_{hf}"

                                def f_u(kslot=kslot, t=t, hf=hf, bk=bk):
                                    return nc.tensor.matmul(P[bk][:, :], kp[:, kslot, hf * 128:(hf + 1) * 128], vv[:, t, :],
                                                            start=True, stop=True)
                                S.op("pe", f_u, reads=["kp", "vv"], writes=[f"P{bk}"])
                                upd.append((bk, sdst, sres, hf))
                        bo = 1 + sl

                        def f_o(t=t, sl=sl, bo=bo, zero_state=zero_state, variants=variants):
                            mm = [(sTm[sl][:], vv[:, t, :])]
                            if not zero_state:
                                for (var, qcol, _k, vi) in variants:
                                    for hf in range(2):
                                        mm.append((qpT[:, hf, qcol:qcol + 128], Sbf[:, (vi if tiles[t]["kind"] == 1 else t % 2), hf, :]))
                            for c, (l, r) in enumerate(mm):
                                ins = nc.tensor.matmul(P[bo][:, :], l, r, start=(c == 0), stop=(c == len(mm) - 1))
                            return ins
                        S.op("pe", f_o, reads=[f"sTm{sl}", "vv", "qpT"] + (["Sbf0", "Sbf1"] if kind == 1 else [f"Sbf{t % 2}"]), writes=[f"P{bo}"])
                        for (bk, sdst, sres, hf) in upd:
                            if zero_state:
                                S.op("dve", lambda sdst=sdst, bk=bk: nc.vector.tensor_copy(out=sdst, in_=P[bk][:, :]),
                                     reads=[f"P{bk}"], writes=[sres])
                            else:
                                g128 = float(cdec[kind, h])
                                S.op("dve", lambda sdst=sdst, bk=bk, g128=g128: nc.vector.scalar_tensor_tensor(
                                    out=sdst, in0=sdst, scalar=g128, in1=P[bk][:, :], op0=ALU.mult, op1=ALU.add),
                                    reads=[f"P{bk}", sres], writes=[sres])
                            if kind == 0 and t != last_prompt_tile:
                                ns_ = (t + 1) % 2
                                S.op("act", lambda sdst=sdst, hf=hf, ns_=ns_: nc.scalar.activation(out=Sbf[:, ns_, hf, :], in_=sdst, func=AF.Copy),
                                     reads=[sres], writes=[f"Sbf{ns_}"])
                        if kind == 1:
                            S.dma("sp", "Sst", [(ss_o[tl["seqs"][vi], h].rearrange("(two p) e -> p two e", p=128), Ss[:, tl["seqs"][vi], :, :])
                                                for vi in range(2)], reads=["Ss"] + [f"Ss{tl['seqs'][vi]}_{hf_}" for vi in range(2) for hf_ in range(2)])
                        elif last and t == last_prompt_tile:
                            S.dma("sp", "Sst", [(sp_o[h].rearrange("(two p) e -> p two e", p=128), St[:, h, :, :])],
                                  reads=[f"St{h}_0", f"St{h}_1"])
                        col = (h * 8 + t) % 32
                        S.op("act", lambda bo=bo, col=col: nc.scalar.activation(out=junk[:], in_=P[bo][:, :], func=AF.Square,
                                                                                 accum_out=stat[:, col:col + 1]),
                             reads=[f"P{bo}"], writes=["junk", f"ss{col}"])
                        S.op("pool", lambda col=col: nc.gpsimd.tensor_scalar(out=stat[:, 32 + col:33 + col], in0=stat[:, col:col + 1],
                                                                             scalar1=1.0 / DV, scalar2=RMS_EPS, op0=ALU.mult, op1=ALU.add),
                             reads=[f"ss{col}"], writes=[f"rs{col}"])
                        S.op("pool", lambda col=col: nc.gpsimd.tensor_tensor(out=stat[:, 32 + col:33 + col], in0=stat[:, 32 + col:33 + col],
                                                                             in1=mhalf[:], op=ALU.pow),
                             reads=[f"rs{col}", "mhalf"], writes=[f"rs{col}"])
                        if t > 1:
                            post_tr(t - 2)
                    for t_ in range(max(0, nt - 2), nt):
                        post_og(t_)
                        deferred.append(lambda t_=t_: post_tr(t_))
                add([(0, 8, 512, w_in_v[:, :, OG + h * 512: OG + (h + 1) * 512])], ph_G)

            for cp in range(4):
                def ph_C(wb, cp=cp):
                    for ci in range(2):
                        c = 2 * cp + ci
                        u = ub[c % 2]
                        ur = f"ub{c % 2}"
                        for sgd in segs:
                            if sgd["kind"] == 1:
                                u3 = u[:, sgd["ubase"]:sgd["ubase"] + sgd["nseq"] * (sgd["L"] + 2)].rearrange("p (s l) -> p s l", s=sgd["nseq"])
                                S.op("dve", lambda u3=u3, c=c, sgd=sgd: nc.vector.tensor_copy(
                                    out=u3[:, :, 0:2], in_=haloT[:, c, :].rearrange("p (s r) -> p s r", s=sgd["nseq"])),
                                    reads=["haloT"], writes=[ur])
                            elif first:
                                S.op("dve", lambda u=u, sgd=sgd: nc.vector.memset(u[:, sgd["ubase"]:sgd["ubase"] + 2], 0.0), writes=[ur])
                            else:
                                S.op("dve", lambda u=u, sgd=sgd, c=c: nc.vector.tensor_copy(out=u[:, sgd["ubase"]:sgd["ubase"] + 2], in_=ucar[:, c, :]),
                                     reads=["ucar"], writes=[ur])
                        for ic, (tok0, ntok, sg) in enumerate(chunks):
                            sgd = segs[sg]
                            tk = slice(tok0, tok0 + ntok)
                            b0 = 0 if (ci * len(chunks) + ic) % 2 == 0 else 3
                            for j in (2, 1, 0):
                                def f(j=j, tk=tk, bk=b0 + j, ci=ci, ntok=ntok):
                                    for kc in range(8):
                                        ins = nc.tensor.matmul(P[bk][:, 0:ntok], wblk(wb, j)[:, kc, ci * 128: ci * 128 + 128],
                                                               xT[:, kc, tk], start=(kc == 0), stop=(kc == 7))
                                    return ins
                                S.op("pe", f, reads=["W"] + xTn(tok0, ntok), writes=[f"P{b0 + j}"])
                            run_deferred()
                            cxs = scr[0][:, 0:ntok]
                            acc = scr[1][:, 0:ntok]
                            S.op("act", lambda cxs=cxs, b0=b0, ntok=ntok: nc.scalar.activation(out=cxs, in_=P[b0 + 2][:, 0:ntok], func=AF.Copy),
                                 reads=[f"P{b0 + 2}"], writes=["scr0"])
                            if sgd["kind"] == 0:
                                lt0 = tok0 - sgd["tok0"]

                                def uview(sh, u=u, sgd=sgd, lt0=lt0, ntok=ntok):
                                    return u[:, sgd["ubase"] + lt0 + sh: sgd["ubase"] + lt0 + sh + ntok]
                                pv = lambda a_: a_
                            else:
                                assert tok0 == sgd["tok0"] and ntok == sgd["nseq"] * sgd["L"]
                                u3 = u[:, sgd["ubase"]:sgd["ubase"] + sgd["nseq"] * (sgd["L"] + 2)].rearrange("p (s l) -> p s l", s=sgd["nseq"])

                                def uview(sh, u3=u3, sgd=sgd):
                                    return u3[:, :, sh:sh + sgd["L"]]
                                pv = lambda a_, sgd=sgd: a_.rearrange("p (s l) -> p s l", s=sgd["nseq"])
                            S.op("dve", lambda: nc.vector.tensor_tensor(
                                out=uview(2), in0=pv(P[b0 + 1][:, 0:ntok]), in1=pv(cxs), op=ALU.mult),
                                reads=[f"P{b0 + 1}", "scr0"], writes=[ur])
                            S.op("dve", lambda: nc.vector.tensor_scalar(
                                out=pv(acc), in0=uview(2), scalar1=wcs[:, 2, c:c + 1], scalar2=None, op0=ALU.mult),
                                reads=[ur, "wcs"], writes=["scr1"])
                            for tap in (1, 0):
                                S.op("dve", lambda tap=tap: nc.vector.scalar_tensor_tensor(
                                    out=pv(acc), in0=uview(tap), scalar=wcs[:, tap, c:c + 1], in1=pv(acc), op0=ALU.mult, op1=ALU.add),
                                    reads=[ur, "wcs", "scr1"], writes=["scr1"])
                            S.op("dve", lambda: nc.vector.tensor_tensor(
                                out=cvT[:, c, tk], in0=P[b0][:, 0:ntok], in1=acc, op=ALU.mult),
                                reads=[f"P{b0}", "scr1"], writes=["cvT"])
                        for sgd in segs:
                            if sgd["kind"] == 0:
                                e0 = sgd["ubase"] + sgd["L"]
                                S.op("act", lambda u=u, e0=e0, c=c: nc.scalar.activation(out=ucar[:, c, :], in_=u[:, e0:e0 + 2], func=AF.Copy),
                                     reads=[ur], writes=["ucar"])
                            else:
                                u3 = u[:, sgd["ubase"]:sgd["ubase"] + sgd["nseq"] * (sgd["L"] + 2)].rearrange("p (s l) -> p s l", s=sgd["nseq"])
                                S.op("act", lambda u3=u3, c=c, sgd=sgd: nc.scalar.activation(
                                    out=scT[:, c, :].rearrange("p (s r) -> p s r", s=sgd["nseq"]), in_=u3[:, :, sgd["L"]:sgd["L"] + 2], func=AF.Copy),
                                    reads=[ur], writes=["scT"])
                    if cp == 3 and has_sample:
                        flush_rows(lambda c0: scT[:, c0:c0 + 4, :], 8, cs_o.rearrange("s r d -> (s r) d"))
                    if cp == 3 and last:
                        flush_rows(lambda c0: ucar[:, c0:c0 + 4, :], 2, cp_o[:, :])
                add([(0, 8, 256, w_in_v[:, :, OCB + cp * 256: OCB + (cp + 1) * 256]),
                     (2048, 8, 256, w_in_v[:, :, OCC + cp * 256: OCC + (cp + 1) * 256]),
                     (4096, 8, 256, w_in_v[:, :, OCX + cp * 256: OCX + (cp + 1) * 256])], ph_C)

            hold = {}
            for cp in range(4):
                def ph_B1(wb, cp=cp):
                    hold["wa"] = wb[:, 0:4096].rearrange("p (k n) -> p k n", k=16)
                add([(0, 16, 256, w_ro_v[:, :, cp * 256:(cp + 1) * 256])], ph_B1)

                def ph_B2(wb, cp=cp):
                    wa = hold["wa"]
                    it = 0
                    for ci in range(2):
                        c = 2 * cp + ci
                        cc_ = slice(ci * 128, (ci + 1) * 128)
                        for (tok0, ntok, sg) in chunks:
                            tk = slice(tok0, tok0 + ntok)
                            b0 = 0 if it % 2 == 0 else 4
                            it += 1

                            def f_r(tk=tk, b0=b0, cc_=cc_, ntok=ntok):
                                for ec in range(16):
                                    ins = nc.tensor.matmul(P[b0][:, 0:ntok], wa[:, ec, cc_], goT[:, ec, tk], start=(ec == 0), stop=(ec == 15))
                                return ins
                            S.op("pe", f_r, reads=["W", "goT"], writes=[f"P{b0}"])
                            for j, src, rname in ((0, cvT, ["cvT"]), (1, xT, xTn(tok0, ntok)), (2, xT, xTn(tok0, ntok))):
                                def f(j=j, src=src, tk=tk, b0=b0, ci=ci, ntok=ntok):
                                    for kc in range(8):
                                        ins = nc.tensor.matmul(P[b0 + 1 + j][:, 0:ntok], wblk(wb, j)[:, kc, ci * 128: ci * 128 + 128],
                                                               src[:, kc, tk], start=(kc == 0), stop=(kc == 7))
                                    return ins
                                S.op("pe", f, reads=["W"] + rname, writes=[f"P{b0 + 1 + j}"])
                            sa_, sb_ = (0, 1) if b0 == 0 else (2, 3)
                            s1, s2 = scr[sa_][:, 0:ntok], scr[sb_][:, 0:ntok]
                            S.op("act", lambda: nc.scalar.activation(out=s1, in_=P[b0 + 2][:, 0:ntok], func=AF.Sigmoid),
                                 reads=[f"P{b0 + 2}"], writes=[f"scr{sa_}"])
                            S.op("act", lambda: nc.scalar.activation(out=s2, in_=P[b0 + 3][:, 0:ntok], func=AF.Sigmoid),
                                 reads=[f"P{b0 + 3}"], writes=[f"scr{sb_}"])
                            S.op("dve", lambda: nc.vector.tensor_tensor(out=s1, in0=P[b0][:, 0:ntok], in1=s1, op=ALU.mult),
                                 reads=[f"P{b0}", f"scr{sa_}"], writes=[f"scr{sa_}"])
                            S.op("dve", lambda: nc.vector.tensor_tensor(out=s2, in0=P[b0 + 1][:, 0:ntok], in1=s2, op=ALU.mult),
                                 reads=[f"P{b0 + 1}", f"scr{sb_}"], writes=[f"scr{sb_}"])
                            S.op("dve", lambda: nc.vector.tensor_tensor(out=mT[:, c, tk], in0=s1, in1=s2, op=ALU.add),
                                 reads=[f"scr{sa_}", f"scr{sb_}"], writes=["mT"])
                add([(0, 8, 256, w_co_v[:, :, cp * 256:(cp + 1) * 256]),
                     (2048, 8, 256, w_in_v[:, :, OGR + cp * 256: OGR + (cp + 1) * 256]),
                     (4096, 8, 256, w_in_v[:, :, OGC + cp * 256: OGC + (cp + 1) * 256])], ph_B2)

            def ln_a1(src_res, src, col):
                sbn = stat[:, 0:12].rearrange("p (a b) -> p a b", a=2)
                c0 = 12 + 4 * (col % 8)
                for a in range(2):
                    S.op("dve", lambda a=a: nc.vector.bn_stats(out=sbn[:, a, :], in_=src[:, a * 512:(a + 1) * 512]),
                         reads=[src_res], writes=[f"bn{a}"])
                S.op("dve", lambda: nc.vector.bn_aggr(out=stat[:, c0:c0 + 2], in_=stat[:, 0:12]),
                     reads=["bn0", "bn1"], writes=[f"mv{c0}"])
                S.op("dve", lambda: nc.vector.tensor_scalar(out=stat[:, c0 + 2:c0 + 3], in0=stat[:, c0 + 1:c0 + 2],
                                                            scalar1=LN_EPS, scalar2=None, op0=ALU.add),
                     reads=[f"mv{c0}"], writes=[f"rstd{c0}"])
                S.op("pool", lambda: nc.gpsimd.tensor_tensor(out=stat[:, c0 + 2:c0 + 3], in0=stat[:, c0 + 2:c0 + 3], in1=mhalf[:], op=ALU.pow),
                     reads=[f"rstd{c0}", "mhalf"], writes=[f"rstd{c0}"])

            def ln_a2(src_res, src, dst_res, dst, col):
                c0 = 12 + 4 * (col % 8)
                S.op("dve", lambda: nc.vector.scalar_tensor_tensor(out=stat[:, c0 + 3:c0 + 4], in0=stat[:, c0:c0 + 1], scalar=-1.0,
                                                                    in1=stat[:, c0 + 2:c0 + 3], op0=ALU.mult, op1=ALU.mult),
                     reads=[f"mv{c0}", f"rstd{c0}"], writes=[f"nmr{c0}"])
                S.op("act", lambda: nc.scalar.activation(out=dst, in_=src, func=AF.Identity,
                                                         scale=stat[:, c0 + 2:c0 + 3], bias=stat[:, c0 + 3:c0 + 4]),
                     reads=[src_res, f"rstd{c0}", f"nmr{c0}"], writes=[dst_res])

            def ln_b(dst_res, dst):
                S.op("dve", lambda: nc.vector.tensor_tensor(out=dst, in0=dst, in1=lnp[0][:], op=ALU.mult),
                     reads=[dst_res, "lnp"], writes=[dst_res])
                S.op("dve", lambda: nc.vector.tensor_tensor(out=dst, in0=dst, in1=lnp[1][:], op=ALU.add),
                     reads=[dst_res, "lnp"], writes=[dst_res])

            def ph_O1(wb):
                hold["wo0"] = wb[:, 0:4096].rearrange("p (k n) -> p k n", k=8)
                S.dma("sp", "lnp", [(lnp[0][:], lnp_d[0]), (lnp[1][:], lnp_d[1])], writes=["lnp"])
            add([(0, 8, 512, w_o_v[:, :, 0:512])], ph_O1)

            def ph_O2(wb):
                wo = [hold["wo0"], wb[:, 0:4096].rearrange("p (k n) -> p k n", k=8)]

                def tr(t):
                    sl = t % 2
                    tt = slice(t * 128, (t + 1) * 128)
                    for hf in range(2):
                        bk = 4 + 2 * sl + hf

                        def f(hf=hf, bk=bk, t=t):
                            for k4 in range(4):
                                kc = hf * 4 + k4
                                ins = nc.tensor.transpose(P[bk][:, k4 * 128:(k4 + 1) * 128], hh[:, t, kc * 128:(kc + 1) * 128], idf[:])
                            return ins
                        S.op("pe", f, reads=[f"hh{t}", "idf"], writes=[f"P{bk}"])
                        for k4 in range(4):
                            kc = hf * 4 + k4
                            if hf == 0:
                                S.op("act", lambda kc=kc, k4=k4, bk=bk, tt=tt: nc.scalar.activation(
                                    out=hT[:, kc, tt], in_=P[bk][:, k4 * 128:(k4 + 1) * 128], func=AF.Identity,
                                    scale=wcs[:, 3, kc:kc + 1], bias=wcs[:, 4, kc:kc + 1]),
                                    reads=[f"P{bk}", "wcs"], writes=[f"hT{t}"])
                            else:
                                S.op("dve", lambda kc=kc, k4=k4, bk=bk, tt=tt: nc.vector.tensor_scalar(
                                    out=hT[:, kc, tt], in0=P[bk][:, k4 * 128:(k4 + 1) * 128],
                                    scalar1=wcs[:, 3, kc:kc + 1], scalar2=wcs[:, 4, kc:kc + 1], op0=ALU.mult, op1=ALU.add),
                                    reads=[f"P{bk}", "wcs"], writes=[f"hT{t}"])

                for t, tl in enumerate(tiles):
                    sl = t % 2
                    tt = slice(t * 128, (t + 1) * 128)
                    S.dma("sp", f"xs{sl}", [(xsb[sl][:], tl["x"])], writes=[f"xs{sl}"])
                    if t > 0:
                        ln_a2(f"hh{t - 1}", hh[:, t - 1, :], f"hh{t - 1}", hh[:, t - 1, :], t - 1)
                    for nh in range(2):
                        bk = 2 * sl + nh

                        def f(nh=nh, bk=bk, tt=tt):
                            for kc in range(8):
                                ins = nc.tensor.matmul(P[bk][:, :], mT[:, kc, tt], wo[nh][:, kc, :], start=(kc == 0), stop=(kc == 7))
                            return ins
                        S.op("pe", f, reads=["W", "mT"], writes=[f"P{bk}"])
                        S.op("dve", lambda nh=nh, bk=bk, t=t, sl=sl: nc.vector.scalar_tensor_tensor(
                            out=hh[:, t, nh * 512:(nh + 1) * 512], in0=xsb[sl][:, nh * 512:(nh + 1) * 512], scalar=ALPHA,
                            in1=P[bk][:, :], op0=ALU.mult, op1=ALU.add),
                            reads=[f"P{bk}", f"xs{sl}"], writes=[f"hh{t}"])
                    ln_a1(f"hh{t}", hh[:, t, :], t)
                    if t > 1:
                        tr(t - 2)
                ln_a2(f"hh{nt - 1}", hh[:, nt - 1, :], f"hh{nt - 1}", hh[:, nt - 1, :], nt - 1)
                tr(max(0, nt - 2))
                deferred.append(lambda: tr(nt - 1))
            add([(0, 8, 512, w_o_v[:, :, 512:1024])], ph_O2)

            for jp in range(NJ // 2):
                def ph_F1(wb, jp=jp):
                    it = 0
                    for ji in range(2):
                        j = 2 * jp + ji
                        for (tok0, ntok, sg) in chunks:
                            tk = slice(tok0, tok0 + ntok)
                            b0 = 2 * (it % 4)
                            it += 1
                            for ab in range(2):
                                def f(ab=ab, tk=tk, b0=b0, ji=ji, ntok=ntok):
                                    for kc in range(8):
                                        ins = nc.tensor.matmul(P[b0 + ab][:, 0:ntok], wblk(wb, ab)[:, kc, ji * 128: ji * 128 + 128],
                                                               hT[:, kc, tk], start=(kc == 0), stop=(kc == 7))
                                    return ins
                                S.op("pe", f, reads=["W"] + [f"hT{t_}" for t_ in range(tok0 // 128, (tok0 + ntok) // 128)], writes=[f"P{b0 + ab}"])
                            if jp == 0 and it == 1:
                                run_deferred()
                            si = (b0 // 2) % 4
                            sa = scr[si][:, 0:ntok]
                            S.op("act", lambda: nc.scalar.activation(out=sa, in_=P[b0][:, 0:ntok], func=AF.Silu),
                                 reads=[f"P{b0}"], writes=[f"scr{si}"])
                            S.op("dve", lambda: nc.vector.tensor_tensor(out=zT[:, j, tk], in0=P[b0 + 1][:, 0:ntok], in1=sa, op=ALU.mult),
                                 reads=[f"P{b0 + 1}", f"scr{si}"], writes=["zT"])
                add([(0, 8, 256, w_up_v[:, :, jp * 256:(jp + 1) * 256]),
                     (2048, 8, 256, w_up_v[:, :, D_FF + jp * 256: D_FF + (jp + 1) * 256])], ph_F1)

            def f2_mm(w, t, bk):
                tt = slice(t * 128, (t + 1) * 128)

                def f():
                    for j in range(NJ):
                        ins = nc.tensor.matmul(P[bk][:, 0:256], zT[:, j, tt], w[:, j, :], start=(j == 0), stop=(j == NJ - 1))
                    return ins
                S.op("pe", f, reads=["W", "zT"], writes=[f"P{bk}"])

            for q in range(2):
                def ph_F2a(wb, q=q):
                    w = wb[:, 0:5632].rearrange("p (k n) -> p k n", k=NJ)
                    for t in range(nt):
                        bk = t % 8
                        f2_mm(w, t, bk)
                        for qq, with_psum in ((q, True), (q + 2, False)):
                            hs = hh[:, t, qq * 256:(qq + 1) * 256]
                            qs = slice(qq * 256, (qq + 1) * 256)
                            si = (2 * t + (0 if with_psum else 1)) % 4
                            tmp = scr[si][:, 0:256]
                            S.op("dve", lambda hs=hs, tmp=tmp, qs=qs: nc.vector.scalar_tensor_tensor(
                                out=tmp, in0=hs, scalar=ALPHA, in1=lnp[0][:, qs], op0=ALU.mult, op1=ALU.mult),
                                reads=[f"hh{t}", "lnp"], writes=[f"scr{si}"])
                            if with_psum:
                                S.op("dve", lambda tmp=tmp, qs=qs: nc.vector.scalar_tensor_tensor(
                                    out=tmp, in0=lnp[1][:, qs], scalar=ALPHA, in1=tmp, op0=ALU.mult, op1=ALU.add),
                                    reads=[f"scr{si}", "lnp"], writes=[f"scr{si}"])
                                S.op("dve", lambda hs=hs, tmp=tmp, bk=bk: nc.vector.tensor_tensor(out=hs, in0=tmp, in1=P[bk][:, 0:256], op=ALU.add),
                                     reads=[f"P{bk}", f"scr{si}"], writes=[f"hh{t}"])
                            else:
                                S.op("dve", lambda hs=hs, tmp=tmp, qs=qs: nc.vector.scalar_tensor_tensor(
                                    out=hs, in0=lnp[1][:, qs], scalar=ALPHA, in1=tmp, op0=ALU.mult, op1=ALU.add),
                                    reads=[f"scr{si}", "lnp"], writes=[f"hh{t}"])
                add([(0, NJ, 256, w_dn_v[:, :, q * 256:(q + 1) * 256])], ph_F2a)

            def ph_F2b(wb):
                hold["wd2"] = wb[:, 0:5632].rearrange("p (k n) -> p k n", k=NJ)
            add([(0, NJ, 256, w_dn_v[:, :, 512:768])], ph_F2b)

            def ph_F2c(wb):
                ws = [hold["wd2"], wb[:, 0:5632].rearrange("p (k n) -> p k n", k=NJ)]
                S.dma("sp", "lnp", [(lnp[0][:], lnp_d[2]), (lnp[1][:], lnp_d[3])], writes=["lnp"])

                R3f = R3[:].bitcast(F32)

                def stg(t):
                    p_ = t % 2
                    return R3f[:, p_ * 1024:(p_ + 1) * 1024], [f"stg{p_}"]

                def fin(t):
                    dst, names = stg(t)
                    S.op("dve", lambda: nc.vector.tensor_tensor(out=dst, in0=dst, in1=lnp[0][:], op=ALU.mult),
                         reads=names + ["lnp"], writes=names)
                    S.op("dve", lambda: nc.vector.tensor_tensor(out=dst, in0=dst, in1=lnp[1][:], op=ALU.add),
                         reads=names + ["lnp"], writes=names)
                    S.dma("act", f"yo{t % 2}", [(tiles[t]["y"], dst)], reads=names + [f"hT{t_}" for t_ in range(nt)] + ["cvT"])
                for t in range(nt):
                    sl = t % 2
                    for i_, qq in enumerate((2, 3)):
                        bk = (2 * t + i_) % 8
                        f2_mm(ws[i_], t, bk)
                        hs = hh[:, t, qq * 256:(qq + 1) * 256]
                        S.op("dve", lambda hs=hs, bk=bk: nc.vector.tensor_tensor(out=hs, in0=hs, in1=P[bk][:, 0:256], op=ALU.add),
                             reads=[f"P{bk}", f"hh{t}"], writes=[f"hh{t}"])
                    ln_a1(f"hh{t}", hh[:, t, :], t)
                    if t > 0:
                        fin(t - 1)
                    dst, names = stg(t)
                    c0 = 12 + 4 * (t % 8)
                    S.op("dve", lambda: nc.vector.scalar_tensor_tensor(out=stat[:, c0 + 3:c0 + 4], in0=stat[:, c0:c0 + 1], scalar=-1.0,
                                                                        in1=stat[:, c0 + 2:c0 + 3], op0=ALU.mult, op1=ALU.mult),
                         reads=[f"mv{c0}", f"rstd{c0}"], writes=[f"nmr{c0}"])
                    S.op("act", lambda: nc.scalar.activation(out=dst, in_=hh[:, t, :], func=AF.Identity,
                                                             scale=stat[:, c0 + 2:c0 + 3], bias=stat[:, c0 + 3:c0 + 4]),
                         reads=[f"hh{t}", f"rstd{c0}", f"nmr{c0}"], writes=names)
                fin(nt - 1)
            add([(0, NJ, 256, w_dn_v[:, :, 768:1024])], ph_F2c)

        def ptile(r0):
            return dict(kind=0, x=xp[r0:r0 + 128, :], y=yp[r0:r0 + 128, :])

        def stile(i):
            return dict(kind=1, x=xsm[i * 128:(i + 1) * 128, :], y=ys[i * 128:(i + 1) * 128, :], seqs=(2 * i, 2 * i + 1))
        NS = NSAMP * DEC_SEQ
        emit_st([stile(0), stile(1)] + [ptile(128 * i) for i in range(4)],
                [(0, 256, 0), (256, 512, 1)],
                [dict(kind=1, tok0=0, L=DEC_SEQ, nseq=NSAMP, ubase=0), dict(kind=0, tok0=256, L=512, nseq=1, ubase=NSAMP * (DEC_SEQ + 2))],
                0, True, False)
        emit_st([ptile(512 + 128 * i) for i in range(6)], [(0, 384, 0), (384, 384, 0)],
                [dict(kind=0, tok0=0, L=768, nseq=1, ubase=0)], NS + 512, False, False)
        emit_st([ptile(1280 + 128 * i) for i in range(6)], [(0, 384, 0), (384, 384, 0)],
                [dict(kind=0, tok0=0, L=768, nseq=1, ubase=0)], NS + 1280, False, True)

        widx = [i for i, (l, _) in enumerate(steps) if l is not None]
        slot_of = {si: k % NWB for k, si in enumerate(widx)}
        emitted = 0

        def emit_load(k):
            si = widx[k]
            sl = slot_of[si]
            pairs = []
            for (off, kk, ncol, src) in steps[si][0]:
                dst = Wb[sl][:, off:off + kk * ncol].rearrange("p (k n) -> p k n", k=kk)
                pairs.append((dst, src))
            S.dma("pool", f"W{sl}", pairs, writes=[f"Wb{sl}"])

        class _Alias:
            cur = []
        orig_sync, orig_record = S._sync, S._record

        def _map(names):
            out = []
            for n in names:
                if n == "W":
                    out.extend(_Alias.cur)
                else:
                    out.append(n)
            return out
        S._sync = lambda eng, r, w: orig_sync(eng, _map(r), _map(w))
        S._record = lambda tok, r, w: orig_record(tok, _map(r), _map(w))

        prev_w = None
        for si, (loads, fn) in enumerate(steps):
            if loads is not None:
                k = widx.index(si)
                while emitted < min(len(widx), k + NWB - 1):
                    emit_load(emitted)
                    emitted += 1
                sl = slot_of[si]
                _Alias.cur = [f"Wb{sl}"] + ([f"Wb{prev_w}"] if prev_w is not None else [])
                fn(Wb[sl])
                prev_w = sl
            else:
                _Alias.cur = []
                fn(None)
            if stop_after is not None and si == stop_after:
                S.barrier()
                d1 = nc.dram_tensor("dbg_R1", [128, R1N], BF16, kind="ExternalOutput").ap()
                d2 = nc.dram_tensor("dbg_R2", [128, NTL * 1024], F32, kind="ExternalOutput").ap()
                d3 = nc.dram_tensor("dbg_R3", [128, 8 * TS], BF16, kind="ExternalOutput").ap()
                S.dma("sp", "dbg", [(d1[:, :], R1[:]), (d2[:, :], R2[:]), (d3[:, :], R3[:])])
                break
        S.finish()
    return nc


_CACHE = {}


def kernel(x_prompt, x_sample, state_ret, cache_conv, w_in, w_conv, w_ret_out, w_conv_out,
           w_o, ln1_g, ln1_b, w_up, w_down, ln2_g, ln2_b):
    f32 = np.float32
    consts, cdec = _constants()
    if "nc" not in _CACHE:
        _CACHE["nc"] = build_program(cdec)
    nc = _CACHE["nc"]

    A = lambda a: np.ascontiguousarray(np.asarray(a, dtype=f32))
    x_prompt, x_sample = A(x_prompt), A(x_sample)
    state_ret, cache_conv = A(state_ret), A(cache_conv)
    wc = A(np.concatenate([np.asarray(w_conv)[0].reshape(3, 8, 128).transpose(2, 0, 1).reshape(128, 24),
                           np.asarray(ln1_g)[0].reshape(8, 128).T, np.asarray(ln1_b)[0].reshape(8, 128).T], axis=1))
    lnp = A(np.stack([np.broadcast_to(np.asarray(v)[0][None, :], (128, D)) for v in (ln1_g, ln1_b, ln2_g, ln2_b)]))
    shared = dict(w_in=A(w_in)[0], w_ro=A(w_ret_out)[0], w_co=A(w_conv_out)[0], w_o=A(w_o)[0],
                  w_up=A(w_up)[0], w_dn=A(w_down)[0], wc=wc, lnp=lnp, **consts)
    in_maps = []
    for c in range(NCORES):
        m = dict(shared)
        m["xp"] = x_prompt[c]
        m["xs"] = x_sample[NSAMP * c:NSAMP * (c + 1)].reshape(NSAMP * DEC_SEQ, D)
        m["sret"] = state_ret[0, NSAMP * c:NSAMP * (c + 1)]
        m["cconv"] = cache_conv[0, NSAMP * c:NSAMP * (c + 1)]
        in_maps.append(m)
    res = run_bass_kernel_spmd(nc, in_maps, core_ids=list(range(NCORES)))
    R = res.results
    y_p = np.stack([R[c]["yp"] for c in range(NCORES)]).astype(f32)
    y_s = np.concatenate([R[c]["ys"].reshape(NSAMP, DEC_SEQ, D) for c in range(NCORES)]).astype(f32)
    s_p = np.stack([R[c]["sp_o"] for c in range(NCORES)])[None].astype(f32)
    c_p = np.stack([R[c]["cp_o"] for c in range(NCORES)])[None].astype(f32)
    s_s = np.concatenate([R[c]["ss_o"] for c in range(NCORES)])[None].astype(f32)
    c_s = np.concatenate([R[c]["cs_o"] for c in range(NCORES)])[None].astype(f32)
    return (y_p, y_s, s_p, c_p, s_s, c_s)
```

```python
import math
from contextlib import ExitStack

import numpy as np
import concourse.bass as bass
import concourse.mybir as mybir
from concourse.bass_utils import run_bass_kernel_spmd

F32 = mybir.dt.float32
BF16 = mybir.dt.bfloat16
AF = mybir.ActivationFunctionType
ALU = mybir.AluOpType

D = 1024
SEQ = 2048
DEC_SEQ = 64
NSAMP = 4
H = 4
DK = 256
DV = 512
D_IN = 11264
D_FF = 2816
NJ = D_FF // 128
LN_EPS = 1e-5
RMS_EPS = 1e-6
ALPHA = 2.0 ** 0.25
PAST = 2048
NCORES = 8

OQ, OK_, OV, OG, OCB, OCC, OCX, OGR, OGC = 0, 1024, 2048, 4096, 6144, 7168, 8192, 9216, 10240


class Sched:
    def __init__(self, nc, es):
        self.nc = nc
        self.es = es
        self.E = {"pe": nc.tensor, "act": nc.scalar, "dve": nc.vector, "pool": nc.gpsimd, "sp": nc.sync}
        self.sem = {k: es.enter_context(nc.semaphore("sem_" + k)) for k in ("pe", "act", "dve", "pool")}
        self.cnt = {k: 0 for k in self.sem}
        self.dsem = {}
        self.dcnt = {}
        self.seen = {k: {} for k in self.E}
        self.lw = {}
        self.rd = {}

    def _wait(self, eng, tok):
        name, sem, val = tok
        if self.seen[eng].get(name, 0) >= val:
            return
        self.E[eng].wait_ge(sem, val)
        self.seen[eng][name] = val

    def _sync(self, eng, reads, writes):
        toks = []
        for r in reads:
            toks.extend(self.lw.get(r, {}).values())
        for w in writes:
            for t in self.lw.get(w, {}).values():
                if t[0] != eng or (eng != "pe" and w in ("junk",)):
                    toks.append(t)
            for t in self.rd.get(w, {}).values():
                if t[0] != eng or eng != "pe":
                    toks.append(t)
        for t in toks:
            self._wait(eng, t)

    def _record(self, tok, reads, writes):
        for r in reads:
            self.rd.setdefault(r, {})[tok[0]] = tok
        for w in writes:
            self.lw.setdefault(w, {})[tok[0]] = tok
            self.rd[w] = {}

    def op(self, eng, fn, reads=(), writes=()):
        self._sync(eng, reads, writes)
        inst = fn()
        self.cnt[eng] += 1
        inst.then_inc(self.sem[eng], 1)
        tok = (eng, self.sem[eng], self.cnt[eng])
        self._record(tok, reads, writes)
        return tok

    def dma(self, q, key, pairs, reads=(), writes=()):
        if key not in self.dsem:
            self.dsem[key] = self.es.enter_context(self.nc.semaphore("dsem_" + key))
            self.dcnt[key] = 0
        self._sync(q, reads, writes)
        for (o, i) in pairs:
            self.E[q].dma_start(out=o, in_=i).then_inc(self.dsem[key], 16)
            self.dcnt[key] += 16
        tok = ("d_" + key, self.dsem[key], self.dcnt[key])
        self._record(tok, reads, writes)
        return tok

    def barrier(self):
        for e in self.E:
            for k in self.sem:
                if k != e and self.cnt[k] > 0:
                    self._wait(e, (k, self.sem[k], self.cnt[k]))
            for k in self.dsem:
                if self.dcnt[k] > 0:
                    self._wait(e, ("d_" + k, self.dsem[k], self.dcnt[k]))

    def finish(self):
        for k in self.sem:
            if self.cnt[k] > 0:
                self._wait("sp", (k, self.sem[k], self.cnt[k]))
        for k in self.dsem:
            if self.dcnt[k] > 0:
                self._wait("sp", ("d_" + k, self.dsem[k], self.dcnt[k]))


def _constants():
    f32 = np.float32
    half = DK // 2
    inv = 1.0 / (10000.0 ** np.linspace(0.0, 1.0, half, dtype=np.float64))
    pos = np.concatenate([np.tile(PAST + np.arange(DEC_SEQ), NSAMP), np.arange(SEQ)]).astype(np.float64)
    ang = pos[None, :] * inv[:, None]
    cs = np.stack([np.cos(ang), np.sin(ang)], axis=1).astype(f32)

    lg = np.log(1.0 - 2.0 ** (-5.0 - np.arange(H, dtype=np.float64)))
    i = np.arange(128)
    decq = np.zeros((3, H, 128), np.float64)
    updk = np.zeros((128, 3, H), np.float64)
    for h in range(H):
        decq[0, h] = np.exp((i + 1) * lg[h])
        decq[1, h, :64] = np.exp((i[:64] + 1) * lg[h])
        decq[2, h, 64:] = np.exp((i[64:] - 64 + 1) * lg[h])
        updk[:, 0, h] = np.exp((127 - i) * lg[h]) / 16.0
        updk[:64, 1, h] = np.exp((63 - i[:64]) * lg[h]) / 16.0
        updk[64:, 2, h] = np.exp((127 - i[64:]) * lg[h]) / 16.0
    maskT = np.zeros((128, 2, H, 128), np.float64)
    jj, ii = np.meshgrid(i, i, indexing="ij")
    same = (jj // 64) == (ii // 64)
    for h in range(H):
        dec_p = np.exp((ii + 1) * lg[h])
        dec_s = np.exp(((ii % 64) + 1) * lg[h])
        intra = np.exp(np.abs(ii - jj) * lg[h])
        cross = np.exp((ii - jj) * lg[h])
        mp = np.where(same, intra, np.where(ii > jj, cross, 0.0))
        maskT[:, 0, h, :] = mp / dec_p / 16.0
        maskT[:, 1, h, :] = np.where(same, intra, 0.0) / dec_s / 16.0
    decq_b = np.broadcast_to(decq.reshape(1, 3 * H * 128), (128, 3 * H * 128))
    cdec = np.exp(np.array([128.0, 64.0])[:, None] * lg[None, :])
    return dict(
        cs=np.ascontiguousarray(cs),
        decq=np.ascontiguousarray(decq_b, dtype=f32),
        updk=np.ascontiguousarray(updk.reshape(128, 3 * H), dtype=f32),
        maskT=np.ascontiguousarray(maskT.reshape(128, 2 * H * 128), dtype=f32),
        ident=np.eye(128, dtype=f32),
    ), cdec


def build_program(cdec, stop_after=None):
    nc = bass.Bass("TRN2", target_bir_lowering=False)

    def din(name, shape):
        return nc.dram_tensor(name, list(shape), F32, kind="ExternalInput").ap()

    def dout(name, shape):
        return nc.dram_tensor(name, list(shape), F32, kind="ExternalOutput").ap()

    xp = din("xp", (SEQ, D))
    xsm = din("xs", (NSAMP * DEC_SEQ, D))
    sret = din("sret", (NSAMP, H, DK, DV))
    cconv = din("cconv", (NSAMP, 2, D))
    w_in = din("w_in", (D, D_IN))
    w_ro = din("w_ro", (2 * D, D))
    w_co = din("w_co", (D, D))
    w_o = din("w_o", (D, D))
    w_up = din("w_up", (D, 2 * D_FF))
    w_dn = din("w_dn", (D_FF, D))
    wc_d = din("wc", (128, 40))
    lnp_d = din("lnp", (4, 128, D))
    cs_d = din("cs", (128, 2, NSAMP * DEC_SEQ + SEQ))
    decq_d = din("decq", (128, 3 * H * 128))
    updk_d = din("updk", (128, 3 * H))
    mask_d = din("maskT", (128, 2 * H * 128))
    ident_d = din("ident", (128, 128))

    yp = dout("yp", (SEQ, D))
    ys = dout("ys", (NSAMP * DEC_SEQ, D))
    sp_o = dout("sp_o", (H, DK, DV))
    cp_o = dout("cp_o", (2, D))
    ss_o = dout("ss_o", (NSAMP, H, DK, DV))
    cs_o = dout("cs_o", (NSAMP, 2, D))

    w_in_v = w_in.rearrange("(kc p) n -> p kc n", p=128)
    w_ro_v = w_ro.rearrange("(kc p) n -> p kc n", p=128)
    w_co_v = w_co.rearrange("(kc p) n -> p kc n", p=128)
    w_o_v = w_o.rearrange("(kc p) n -> p kc n", p=128)
    w_up_v = w_up.rearrange("(kc p) n -> p kc n", p=128)
    w_dn_v = w_dn.rearrange("(kc p) n -> p kc n", p=128)

    TS = 768
    NTL = TS // 128
    es = ExitStack()
    with es:
        def sb(name, shape, dt):
            return es.enter_context(nc.sbuf_tensor("sb_" + name, list(shape), dt))

        R1N = 8 * TS + 2 * 1024 + 2 * TS + 8 * 256 + NTL * 512 * 2
        R1 = sb("R1", (128, R1N), BF16)
        R2 = sb("R2", (128, NTL * 1024), F32)
        R3 = sb("R3", (128, 8 * TS), BF16)
        St = sb("St", (128, 4, 2, 512), F32)
        Ss = sb("Ss", (128, 4, 2, 512), F32)
        Sbf = sb("Sbf", (128, 2, 2, 512), BF16)
        cst = sb("cst", (128, 2, TS), F32)
        xsb = [sb("xs0", (128, 1024), F32), sb("xs1", (128, 1024), F32)]
        NWB = 4
        Wb = [sb(f"W{i}", (128, 6144), BF16) for i in range(NWB)]
        lnp = [sb("lnp0", (128, 1024), F32), sb("lnp1", (128, 1024), F32)]
        UBN = 780
        ub = [sb("ub0", (128, UBN), F32), sb("ub1", (128, UBN), F32)]
        scrT = sb("scr", (128, 2048), F32)
        scr = [scrT[:, i * 512:(i + 1) * 512] for i in range(4)]
        maskt = sb("maskt", (128, 2, H, 128), F32)
        decq = sb("decq", (128, 3, H, 128), F32)
        updk = sb("updk", (128, 3, H), F32)
        wcs = sb("wcs", (128, 5, 8), F32)
        idf = sb("idf", (128, 128), F32)
        idb = sb("idb", (128, 128), BF16)
        ucar = sb("ucar", (128, 8, 2), F32)
        haloT = sb("haloT", (128, 8, 8), F32)
        scT = sb("scT", (128, 8, 8), F32)
        sTm = [sb("sTm0", (128, 128), BF16), sb("sTm1", (128, 128), BF16)]
        ogb = [sb("og0", (128, 512), BF16), sb("og1", (128, 512), BF16)]
        junk = sb("junk", (128, 512), BF16)
        stat = sb("stat", (128, 64), F32)
        mhalf = sb("mhalf", (128, 1), F32)
        P = [es.enter_context(nc.psum_tensor(f"P{i}", [128, 512], F32)) for i in range(8)]

        S = Sched(nc, es)

        def v3(t, off, a, b):
            return t[:, off:off + a * b].rearrange("p (a b) -> p a b", a=a)

        o_ = 0
        xT = v3(R1, o_, 8, TS); o_ += 8 * TS
        HB0 = o_
        qpT = v3(R1, o_, 2, 1024); o_ += 2048
        kT = v3(R1, o_, 2, TS); o_ += 2 * TS
        kp = v3(R1, o_, 8, 256); o_ += 2048
        vv = v3(R1, o_, NTL, 512); o_ += NTL * 512
        gg = v3(R1, o_, NTL, 512); o_ += NTL * 512
        assert o_ == R1N and R1N >= 22 * TS and R1N - HB0 >= 8 * TS
        mT = v3(R1, HB0, 8, TS)
        zT = v3(R1, 0, 22, TS)
        hh = v3(R2, 0, NTL, 1024)
        goT = R2[:].bitcast(BF16)[:, 0:16 * TS].rearrange("p (a b) -> p a b", a=16)
        cvT = v3(R3, 0, 8, TS)
        hT = v3(R3, 0, 8, TS)

        def wblk(wb, j):
            return wb[:, j * 2048:(j + 1) * 2048].rearrange("p (k n) -> p k n", k=8)

        S.dma("sp", "misc", [(maskt[:].rearrange("p a h i -> p (a h i)"), mask_d[:, :]),
                             (decq[:].rearrange("p a h i -> p (a h i)"), decq_d[:, :]),
                             (updk[:].rearrange("p a h -> p (a h)"), updk_d[:, :]),
                             (wcs[:].rearrange("p a c -> p (a c)"), wc_d[:, :]),
                             (idf[:], ident_d[:, :])],
              writes=["maskt", "decq", "updk", "wcs", "idf"])
        S.dma("pool", "misc2", [(idb[:], ident_d[:, :])], writes=["idb"])
        S.op("pool", lambda: nc.gpsimd.memset(mhalf[:], -0.5), writes=["mhalf"])

        hin = xsb[1][0:8, :]
        S.op("dve", lambda: nc.vector.memset(xsb[1][:], 0.0), writes=["xs1"])
        S.dma("sp", "xs1", [(hin, cconv.rearrange("s r d -> (s r) d"))], writes=["xs1"])

        for hf in range(2):
            def f_h(hf=hf):
                for k4 in range(4):
                    c = hf * 4 + k4
                    ins = nc.tensor.transpose(P[hf][:, k4 * 128:(k4 + 1) * 128], xsb[1][:, c * 128:(c + 1) * 128], idf[:])
                return ins
            S.op("pe", f_h, reads=["xs1", "idf"], writes=[f"P{hf}"])
            S.op("dve", lambda hf=hf: nc.vector.tensor_copy(out=haloT[:, hf * 4:(hf + 1) * 4, :],
                                                            in_=P[hf][:].rearrange("p (a b) -> p a b", a=4)[:, :, 0:8]),
                 reads=[f"P{hf}"], writes=["haloT"])

        def flush_rows(src_fn, nrows, dst):
            for hf in range(2):
                stg = scr[hf][:, 0:512].rearrange("p (a b) -> p a b", a=4)
                S.op("dve", lambda hf=hf, stg=stg: nc.vector.tensor_copy(out=stg[:, :, 0:nrows], in_=src_fn(hf * 4)),
                     reads=["ucar", "scT"], writes=[f"scr{hf}"])

                def f(hf=hf):
                    for k4 in range(4):
                        ins = nc.tensor.transpose(P[6 + hf][:, k4 * 128:(k4 + 1) * 128], scr[hf][:, k4 * 128:(k4 + 1) * 128], idf[:])
                    return ins
                S.op("pe", f, reads=[f"scr{hf}", "idf"], writes=[f"P{6 + hf}"])
                S.op("act", lambda hf=hf: nc.scalar.activation(out=hin[0:nrows, hf * 512:(hf + 1) * 512], in_=P[6 + hf][0:nrows, :], func=AF.Copy),
                     reads=[f"P{6 + hf}"], writes=["xs1"])
            S.dma("sp", "xs1", [(dst, hin[0:nrows, :])], reads=["xs1"])

        steps = []

        def add(loads, fn):
            steps.append((loads, fn))

        def emit_st(tiles, chunks, segs, col0, first, last):
            nt = len(tiles)
            T = nt * 128
            has_prompt = any(tl["kind"] == 0 for tl in tiles)
            has_sample = any(tl["kind"] == 1 for tl in tiles)
            for t, tl in enumerate(tiles):
                if tl["kind"] == 0:
                    tl["vars"] = [(0, t * 128, t, 0)]
                else:
                    tl["vars"] = [(1, t * 128, t, 0), (2, TS + t * 128, NTL + t, 1)]
            tchunk = {}
            for ci_, (tok0_, ntok_, _sg) in enumerate(chunks):
                for t_ in range(tok0_ // 128, (tok0_ + ntok_) // 128):
                    tchunk[t_] = ci_

            def xTn(tok0, ntok):
                return [f"xT{t_}" for t_ in range(tok0 // 128, (tok0 + ntok) // 128)]
            fpt = [t for t, tl in enumerate(tiles) if tl["kind"] == 0]
            first_prompt_tile = fpt[0] if fpt else None
            last_prompt_tile = fpt[-1] if fpt else None

            def ph_A0(_):
                S.dma("sp", "cs", [(cst[:, :, 0:T], cs_d[:, :, col0:col0 + T])], writes=["cst"])
                xslots = [(xsb[0][:], ["xs0"], "xs0"), (xsb[1][:], ["xs1"], "xs1"),
                          (scrT[:, 0:1024], ["scr0", "scr1"], "xq0"), (scrT[:, 1024:2048], ["scr2", "scr3"], "xq1")]
                for t, tl in enumerate(tiles):
                    xbuf, xnames, xkey = xslots[t % 4]
                    S.dma("sp", xkey, [(xbuf, tl["x"])], writes=xnames)
                    for hf in range(2):
                        bk = (2 * t + hf) % 8

                        def f(bk=bk, hf=hf, xbuf=xbuf):
                            for k4 in range(4):
                                kc = hf * 4 + k4
                                ins = nc.tensor.transpose(P[bk][:, k4 * 128:(k4 + 1) * 128],
                                                          xbuf[:, kc * 128:(kc + 1) * 128], idf[:])
                            return ins
                        S.op("pe", f, reads=xnames + ["idf"], writes=[f"P{bk}"])
                        src = P[bk][:].rearrange("p (a b) -> p a b", a=4)
                        dst = xT[:, hf * 4:hf * 4 + 4, t * 128:(t + 1) * 128]
                        S.op("act", lambda src=src, dst=dst: nc.scalar.activation(out=dst, in_=src, func=AF.Copy),
                             reads=[f"P{bk}"], writes=[f"xT{t}"])
            add(None, ph_A0)

            deferred = []

            def run_deferred():
                while deferred:
                    deferred.pop(0)()

            for h in range(H):
                def ph_QK(wb, h=h):
                    if has_sample:
                        seqs = sorted(s_ for tl in tiles if tl["kind"] == 1 for s_ in tl["seqs"])
                        S.dma("sp", "Sld", [(Ss[:, s_, :, :], sret[s_, h].rearrange("(two p) e -> p two e", p=128))
                                            for s_ in seqs], writes=["Ss"] + [f"Ss{s_}_{hf_}" for s_ in seqs for hf_ in range(2)])
                    if has_prompt and not first:
                        fs_ = first_prompt_tile % 2
                        S.op("act", lambda: nc.scalar.activation(out=Sbf[:, fs_, :, :], in_=St[:, h, :, :], func=AF.Copy),
                             reads=[f"St{h}_0", f"St{h}_1"], writes=[f"Sbf{fs_}"])
                    bi = 0
                    for (tok0, ntok, sg) in chunks:
                        tk = slice(tok0, tok0 + ntok)
                        ckind = segs[sg]["kind"]
                        for which in range(2):
                            b1, b2 = (0, 1) if bi % 2 == 0 else (2, 3)
                            bi += 1
                            for hf, bk in ((0, b1), (1, b2)):
                                def f(bk=bk, hf=hf, which=which, tk=tk, ntok=ntok):
                                    for kc in range(8):
                                        ins = nc.tensor.matmul(P[bk][:, 0:ntok],
                                                               wblk(wb, which)[:, kc, hf * 128: hf * 128 + 128],
                                                               xT[:, kc, tk], start=(kc == 0), stop=(kc == 7))
                                    return ins
                                S.op("pe", f, reads=["W"] + xTn(tok0, ntok), writes=[f"P{bk}"])
                            cos = cst[:, 0, tk]
                            sin = cst[:, 1, tk]
                            ra, rb, ra2, rb2 = (scr[i][:, 0:ntok] for i in range(4))
                            t1, t2 = P[b1][:, 0:ntok], P[b2][:, 0:ntok]
                            if which == 1:
                                o1, o2 = kT[:, 0, tk], kT[:, 1, tk]
                                w1, w2 = [f"kT{tchunk[tok0 // 128]}"], [f"kT{tchunk[tok0 // 128]}"]
                            else:
                                o1, o2 = ub[0][:, 0:ntok], ub[1][:, 0:ntok]
                                w1, w2 = ["ub0"], ["ub1"]
                            S.op("dve", lambda: nc.vector.tensor_tensor(out=ra, in0=t1, in1=cos, op=ALU.mult),
                                 reads=[f"P{b1}", "cst"], writes=["scr0"])
                            S.op("dve", lambda: nc.vector.tensor_tensor(out=rb, in0=t2, in1=sin, op=ALU.mult),
                                 reads=[f"P{b2}", "cst"], writes=["scr1"])
                            S.op("dve", lambda: nc.vector.tensor_tensor(out=ra2, in0=t1, in1=sin, op=ALU.mult),
                                 reads=[f"P{b1}", "cst"], writes=["scr2"])
                            S.op("dve", lambda: nc.vector.tensor_tensor(out=rb2, in0=t2, in1=cos, op=ALU.mult),
                                 reads=[f"P{b2}", "cst"], writes=["scr3"])
                            S.op("dve", lambda: nc.vector.tensor_tensor(out=o1, in0=ra, in1=rb, op=ALU.subtract),
                                 reads=["scr0", "scr1"], writes=w1)
                            S.op("dve", lambda: nc.vector.tensor_tensor(out=o2, in0=ra2, in1=rb2, op=ALU.add),
                                 reads=["scr2", "scr3"], writes=w2)
                            run_deferred()
                            if which == 0:
                                nct = ntok // 128
                                t0i = tok0 // 128
                                vlist = [(0, tok0)] if ckind == 0 else [(1, tok0), (2, TS + tok0)]
                                for (var, qcol) in vlist:
                                    for hf in range(2):
                                        src = ub[hf][:, 0:ntok].rearrange("p (a b) -> p a b", b=128)
                                        dq = decq[:, var, h, :].unsqueeze(1).broadcast_to([128, nct, 128])
                                        dst = qpT[:, hf, qcol:qcol + ntok].rearrange("p (a b) -> p a b", b=128)
                                        S.op("dve", lambda src=src, dq=dq, dst=dst: nc.vector.tensor_tensor(out=dst, in0=src, in1=dq, op=ALU.mult),
                                             reads=[f"ub{hf}", "decq"], writes=["qpT"])
                add([(0, 8, 256, w_in_v[:, :, OQ + h * 256: OQ + (h + 1) * 256]),
                     (2048, 8, 256, w_in_v[:, :, OK_ + h * 256: OK_ + (h + 1) * 256])], ph_QK)

                def ph_V(wb, h=h):
                    w = wb[:, 0:4096].rearrange("p (k n) -> p k n", k=8)
                    for t in range(nt):
                        bk = 1 + (t % 2)

                        def f(t=t, bk=bk):
                            for kc in range(8):
                                ins = nc.tensor.matmul(P[bk][:, :], xT[:, kc, t * 128:(t + 1) * 128], w[:, kc, :],
                                                       start=(kc == 0), stop=(kc == 7))
                            return ins
                        S.op("pe", f, reads=["W", f"xT{t}"], writes=[f"P{bk}"])
                        S.op("act", lambda t=t, bk=bk: nc.scalar.activation(out=vv[:, t, :], in_=P[bk][:, :], func=AF.Copy),
                             reads=[f"P{bk}"], writes=["vv"])
                add([(0, 8, 512, w_in_v[:, :, OV + h * 512: OV + (h + 1) * 512])], ph_V)

                def ph_G(wb, h=h):
                    w = wb[:, 0:4096].rearrange("p (k n) -> p k n", k=8)
                    for t in range(nt):
                        bk = 1 + (t % 2)

                        def f(t=t, bk=bk):
                            for kc in range(8):
                                ins = nc.tensor.matmul(P[bk][:, :], xT[:, kc, t * 128:(t + 1) * 128], w[:, kc, :],
                                                       start=(kc == 0), stop=(kc == 7))
                            return ins
                        S.op("pe", f, reads=["W", f"xT{t}"], writes=[f"P{bk}"])
                        S.op("act", lambda t=t, bk=bk: nc.scalar.activation(out=gg[:, t, :], in_=P[bk][:, :], func=AF.Silu),
                             reads=[f"P{bk}"], writes=["gg"])
                        tl = tiles[t]
                        bk = 4 + (t % 2)
                        pb = P[bk][:].bitcast(BF16)

                        def f(t=t, pb=pb):
                            for hf in range(2):
                                ins = nc.tensor.transpose(pb[:, hf * 128:(hf + 1) * 128], kT[:, hf, t * 128:(t + 1) * 128], idb[:])
                            return ins
                        S.op("pe", f, reads=[f"kT{tchunk[t]}", "idb"], writes=[f"P{bk}"])
                        for (var, _q, kslot, _vi) in tl["vars"]:
                            S.op("act", lambda var=var, kslot=kslot, pb=pb: nc.scalar.activation(
                                out=kp[:, kslot, :], in_=pb[:, 0:256], func=AF.Identity, scale=updk[:, var, h:h + 1]),
                                reads=[f"P{bk}", "updk"], writes=["kp"])
                    def post_og(t):
                        sl = t % 2
                        bo = 1 + sl
                        col = (h * 8 + t) % 32
                        S.op("dve", lambda: nc.vector.scalar_tensor_tensor(
                            out=ogb[sl][:], in0=P[bo][:, :], scalar=stat[:, 32 + col:33 + col], in1=gg[:, t, :],
                            op0=ALU.mult, op1=ALU.mult),
                            reads=[f"P{bo}", f"rs{col}", "gg"], writes=[f"og{sl}"])

                    def post_tr(t):
                        sl = t % 2
                        tt = slice(t * 128, (t + 1) * 128)
                        p7 = P[7][:].bitcast(BF16)[:, 0:512]

                        def f_t():
                            for ec in range(4):
                                ins = nc.tensor.transpose(p7[:, ec * 128:(ec + 1) * 128], ogb[sl][:, ec * 128:(ec + 1) * 128], idb[:])
                            return ins
                        S.op("pe", f_t, reads=[f"og{sl}", "idb"], writes=["P7"])
                        S.op("act", lambda: nc.scalar.activation(
                            out=goT[:, h * 4:(h + 1) * 4, tt], in_=p7.rearrange("p (a b) -> p a b", a=4), func=AF.Copy),
                            reads=["P7"], writes=["goT"])

                    def scores(t):
                        tl = tiles[t]
                        tt = slice(t * 128, (t + 1) * 128)
                        sl = t % 2
                        variants = tl["vars"]

                        def f_s():
                            n_mm = 2 * len(variants)
                            c = 0
                            for (var, qcol, _k, _vi) in variants:
                                for hf in range(2):
                                    ins = nc.tensor.matmul(P[0][:, 0:128], kT[:, hf, tt], qpT[:, hf, qcol:qcol + 128],
                                                           start=(c == 0), stop=(c == n_mm - 1))
                                    c += 1
                            return ins
                        S.op("pe", f_s, reads=[f"kT{tchunk[t]}", "qpT"], writes=["P0"])
                        S.op("dve", lambda: nc.vector.tensor_tensor(out=sTm[sl][:], in0=P[0][:, 0:128], in1=maskt[:, tl["kind"], h, :], op=ALU.mult),
                             reads=["P0", "maskt"], writes=[f"sTm{sl}"])

                    scores(0)
                    for t, tl in enumerate(tiles):
                        kind = tl["kind"]
                        variants = tl["vars"]
                        zero_state = (kind == 0 and first and t == first_prompt_tile)
                        sl = t % 2
                        if t + 1 < nt:
                            scores(t + 1)
                        if kind == 1:
                            for vi in range(2):
                                S.op("act", lambda vi=vi, tl=tl: nc.scalar.activation(out=Sbf[:, vi, :, :], in_=Ss[:, tl["seqs"][vi], :, :], func=AF.Copy),
                                     reads=[f"Ss{tl['seqs'][vi]}_0", f"Ss{tl['seqs'][vi]}_1"], writes=[f"Sbf{vi}"])
                        if t > 1:
                            post_og(t - 2)
                        upd = []
                        for (var, _q, kslot, vi) in variants:
                            for hf in range(2):
                                if kind == 0:
                                    bk = 3 + 2 * sl + hf
                                    sdst = St[:, h, hf, :]
                                    sres = f"St{h}_{hf}"
                                else:
                                    bk = 3 + 2 * vi + hf
                                    sdst = Ss[:, tl["seqs"][vi], hf, :]
                                    sres = f"Ss{tl['seqs'][vi]}_{hf}"

                                def f_u(kslot=kslot, t=t, hf=hf, bk=bk):
                                    return nc.tensor.matmul(P[bk][:, :], kp[:, kslot, hf * 128:(hf + 1) * 128], vv[:, t, :],
                                                            start=True, stop=True)
                                S.op("pe", f_u, reads=["kp", "vv"], writes=[f"P{bk}"])
                                upd.append((bk, sdst, sres, hf))
                        bo = 1 + sl

                        def f_o(t=t, sl=sl, bo=bo, zero_state=zero_state, variants=variants):
                            mm = [(sTm[sl][:], vv[:, t, :])]
                            if not zero_state:
                                for (var, qcol, _k, vi) in variants:
                                    for hf in range(2):
                                        mm.append((qpT[:, hf, qcol:qcol + 128], Sbf[:, (vi if tiles[t]["kind"] == 1 else t % 2), hf, :]))
                            for c, (l, r) in enumerate(mm):
                                ins = nc.tensor.matmul(P[bo][:, :], l, r, start=(c == 0), stop=(c == len(mm) - 1))
                            return ins
                        S.op("pe", f_o, reads=[f"sTm{sl}", "vv", "qpT"] + (["Sbf0", "Sbf1"] if kind == 1 else [f"Sbf{t % 2}"]), writes=[f"P{bo}"])
                        for (bk, sdst, sres, hf) in upd:
                            if zero_state:
                                S.op("dve", lambda sdst=sdst, bk=bk: nc.vector.tensor_copy(out=sdst, in_=P[bk][:, :]),
                                     reads=[f"P{bk}"], writes=[sres])
                            else:
                                g128 = float(cdec[kind, h])
                                S.op("dve", lambda sdst=sdst, bk=bk, g128=g128: nc.vector.scalar_tensor_tensor(
                                    out=sdst, in0=sdst, scalar=g128, in1=P[bk][:, :], op0=ALU.mult, op1=ALU.add),
                                    reads=[f"P{bk}", sres], writes=[sres])
                            if kind == 0 and t != last_prompt_tile:
                                ns_ = (t + 1) % 2
                                S.op("act", lambda sdst=sdst, hf=hf, ns_=ns_: nc.scalar.activation(out=Sbf[:, ns_, hf, :], in_=sdst, func=AF.Copy),
                                     reads=[sres], writes=[f"Sbf{ns_}"])
                        if kind == 1:
                            S.dma("sp", "Sst", [(ss_o[tl["seqs"][vi], h].rearrange("(two p) e -> p two e", p=128), Ss[:, tl["seqs"][vi], :, :])
                                                for vi in range(2)], reads=["Ss"] + [f"Ss{tl['seqs'][vi]}_{hf_}" for vi in range(2) for hf_ in range(2)])
                        elif last and t == last_prompt_tile:
                            S.dma("sp", "Sst", [(sp_o[h].rearrange("(two p) e -> p two e", p=128), St[:, h, :, :])],
                                  reads=[f"St{h}_0", f"St{h}_1"])
                        col = (h * 8 + t) % 32
                        S.op("act", lambda bo=bo, col=col: nc.scalar.activation(out=junk[:], in_=P[bo][:, :], func=AF.Square,
                                                                                 accum_out=stat[:, col:col + 1]),
                             reads=[f"P{bo}"], writes=["junk", f"ss{col}"])
                        S.op("pool", lambda col=col: nc.gpsimd.tensor_scalar(out=stat[:, 32 + col:33 + col], in0=stat[:, col:col + 1],
                                                                             scalar1=1.0 / DV, scalar2=RMS_EPS, op0=ALU.mult, op1=ALU.add),
                             reads=[f"ss{col}"], writes=[f"rs{col}"])
                        S.op("pool", lambda col=col: nc.gpsimd.tensor_tensor(out=stat[:, 32 + col:33 + col], in0=stat[:, 32 + col:33 + col],
                                                                             in1=mhalf[:], op=ALU.pow),
                             reads=[f"rs{col}", "mhalf"], writes=[f"rs{col}"])
                        if t > 1:
                            post_tr(t - 2)
                    for t_ in range(max(0, nt - 2), nt):
                        post_og(t_)
                        deferred.append(lambda t_=t_: post_tr(t_))
                add([(0, 8, 512, w_in_v[:, :, OG + h * 512: OG + (h + 1) * 512])], ph_G)

            for cp in range(4):
                def ph_C(wb, cp=cp):
                    for ci in range(2):
                        c = 2 * cp + ci
                        u = ub[c % 2]
                        ur = f"ub{c % 2}"
                        for sgd in segs:
                            if sgd["kind"] == 1:
                                u3 = u[:, sgd["ubase"]:sgd["ubase"] + sgd["nseq"] * (sgd["L"] + 2)].rearrange("p (s l) -> p s l", s=sgd["nseq"])
                                S.op("dve", lambda u3=u3, c=c, sgd=sgd: nc.vector.tensor_copy(
                                    out=u3[:, :, 0:2], in_=haloT[:, c, :].rearrange("p (s r) -> p s r", s=sgd["nseq"])),
                                    reads=["haloT"], writes=[ur])
                            elif first:
                                S.op("dve", lambda u=u, sgd=sgd: nc.vector.memset(u[:, sgd["ubase"]:sgd["ubase"] + 2], 0.0), writes=[ur])
                            else:
                                S.op("dve", lambda u=u, sgd=sgd, c=c: nc.vector.tensor_copy(out=u[:, sgd["ubase"]:sgd["ubase"] + 2], in_=ucar[:, c, :]),
                                     reads=["ucar"], writes=[ur])
                        for ic, (tok0, ntok, sg) in enumerate(chunks):
                            sgd = segs[sg]
                            tk = slice(tok0, tok0 + ntok)
                            b0 = 0 if (ci * len(chunks) + ic) % 2 == 0 else 3
                            for j in (2, 1, 0):
                                def f(j=j, tk=tk, bk=b0 + j, ci=ci, ntok=ntok):
                                    for kc in range(8):
                                        ins = nc.tensor.matmul(P[bk][:, 0:ntok], wblk(wb, j)[:, kc, ci * 128: ci * 128 + 128],
                                                               xT[:, kc, tk], start=(kc == 0), stop=(kc == 7))
                                    return ins
                                S.op("pe", f, reads=["W"] + xTn(tok0, ntok), writes=[f"P{b0 + j}"])
                            run_deferred()
                            cxs = scr[0][:, 0:ntok]
                            acc = scr[1][:, 0:ntok]
                            S.op("act", lambda cxs=cxs, b0=b0, ntok=ntok: nc.scalar.activation(out=cxs, in_=P[b0 + 2][:, 0:ntok], func=AF.Copy),
                                 reads=[f"P{b0 + 2}"], writes=["scr0"])
                            if sgd["kind"] == 0:
                                lt0 = tok0 - sgd["tok0"]

                                def uview(sh, u=u, sgd=sgd, lt0=lt0, ntok=ntok):
                                    return u[:, sgd["ubase"] + lt0 + sh: sgd["ubase"] + lt0 + sh + ntok]
                                pv = lambda a_: a_
                            else:
                                assert tok0 == sgd["tok0"] and ntok == sgd["nseq"] * sgd["L"]
                                u3 = u[:, sgd["ubase"]:sgd["ubase"] + sgd["nseq"] * (sgd["L"] + 2)].rearrange("p (s l) -> p s l", s=sgd["nseq"])

                                def uview(sh, u3=u3, sgd=sgd):
                                    return u3[:, :, sh:sh + sgd["L"]]
                                pv = lambda a_, sgd=sgd: a_.rearrange("p (s l) -> p s l", s=sgd["nseq"])
                            S.op("dve", lambda: nc.vector.tensor_tensor(
                                out=uview(2), in0=pv(P[b0 + 1][:, 0:ntok]), in1=pv(cxs), op=ALU.mult),
                                reads=[f"P{b0 + 1}", "scr0"], writes=[ur])
                            S.op("dve", lambda: nc.vector.tensor_scalar(
                                out=pv(acc), in0=uview(2), scalar1=wcs[:, 2, c:c + 1], scalar2=None, op0=ALU.mult),
                                reads=[ur, "wcs"], writes=["scr1"])
                            for tap in (1, 0):
                                S.op("dve", lambda tap=tap: nc.vector.scalar_tensor_tensor(
                                    out=pv(acc), in0=uview(tap), scalar=wcs[:, tap, c:c + 1], in1=pv(acc), op0=ALU.mult, op1=ALU.add),
                                    reads=[ur, "wcs", "scr1"], writes=["scr1"])
                            S.op("dve", lambda: nc.vector.tensor_tensor(
                                out=cvT[:, c, tk], in0=P[b0][:, 0:ntok], in1=acc, op=ALU.mult),
                                reads=[f"P{b0}", "scr1"], writes=["cvT"])
                        for sgd in segs:
                            if sgd["kind"] == 0:
                                e0 = sgd["ubase"] + sgd["L"]
                                S.op("act", lambda u=u, e0=e0, c=c: nc.scalar.activation(out=ucar[:, c, :], in_=u[:, e0:e0 + 2], func=AF.Copy),
                                     reads=[ur], writes=["ucar"])
                            else:
                                u3 = u[:, sgd["ubase"]:sgd["ubase"] + sgd["nseq"] * (sgd["L"] + 2)].rearrange("p (s l) -> p s l", s=sgd["nseq"])
                                S.op("act", lambda u3=u3, c=c, sgd=sgd: nc.scalar.activation(
                                    out=scT[:, c, :].rearrange("p (s r) -> p s r", s=sgd["nseq"]), in_=u3[:, :, sgd["L"]:sgd["L"] + 2], func=AF.Copy),
                                    reads=[ur], writes=["scT"])
                    if cp == 3 and has_sample:
                        flush_rows(lambda c0: scT[:, c0:c0 + 4, :], 8, cs_o.rearrange("s r d -> (s r) d"))
                    if cp == 3 and last:
                        flush_rows(lambda c0: ucar[:, c0:c0 + 4, :], 2, cp_o[:, :])
                add([(0, 8, 256, w_in_v[:, :, OCB + cp * 256: OCB + (cp + 1) * 256]),
                     (2048, 8, 256, w_in_v[:, :, OCC + cp * 256: OCC + (cp + 1) * 256]),
                     (4096, 8, 256, w_in_v[:, :, OCX + cp * 256: OCX + (cp + 1) * 256])], ph_C)

            hold = {}
            for cp in range(4):
                def ph_B1(wb, cp=cp):
                    hold["wa"] = wb[:, 0:4096].rearrange("p (k n) -> p k n", k=16)
                add([(0, 16, 256, w_ro_v[:, :, cp * 256:(cp + 1) * 256])], ph_B1)

                def ph_B2(wb, cp=cp):
                    wa = hold["wa"]
                    it = 0
                    for ci in range(2):
                        c = 2 * cp + ci
                        cc_ = slice(ci * 128, (ci + 1) * 128)
                        for (tok0, ntok, sg) in chunks:
                            tk = slice(tok0, tok0 + ntok)
                            b0 = 0 if it % 2 == 0 else 4
                            it += 1

                            def f_r(tk=tk, b0=b0, cc_=cc_, ntok=ntok):
                                for ec in range(16):
                                    ins = nc.tensor.matmul(P[b0][:, 0:ntok], wa[:, ec, cc_], goT[:, ec, tk], start=(ec == 0), stop=(ec == 15))
                                return ins
                            S.op("pe", f_r, reads=["W", "goT"], writes=[f"P{b0}"])
                            for j, src, rname in ((0, cvT, ["cvT"]), (1, xT, xTn(tok0, ntok)), (2, xT, xTn(tok0, ntok))):
                                def f(j=j, src=src, tk=tk, b0=b0, ci=ci, ntok=ntok):
                                    for kc in range(8):
                                        ins = nc.tensor.matmul(P[b0 + 1 + j][:, 0:ntok], wblk(wb, j)[:, kc, ci * 128: ci * 128 + 128],
                                                               src[:, kc, tk], start=(kc == 0), stop=(kc == 7))
                                    return ins
                                S.op("pe", f, reads=["W"] + rname, writes=[f"P{b0 + 1 + j}"])
                            sa_, sb_ = (0, 1) if b0 == 0 else (2, 3)
                            s1, s2 = scr[sa_][:, 0:ntok], scr[sb_][:, 0:ntok]
                            S.op("act", lambda: nc.scalar.activation(out=s1, in_=P[b0 + 2][:, 0:ntok], func=AF.Sigmoid),
                                 reads=[f"P{b0 + 2}"], writes=[f"scr{sa_}"])
                            S.op("act", lambda: nc.scalar.activation(out=s2, in_=P[b0 + 3][:, 0:ntok], func=AF.Sigmoid),
                                 reads=[f"P{b0 + 3}"], writes=[f"scr{sb_}"])
                            S.op("dve", lambda: nc.vector.tensor_tensor(out=s1, in0=P[b0][:, 0:ntok], in1=s1, op=ALU.mult),
                                 reads=[f"P{b0}", f"scr{sa_}"], writes=[f"scr{sa_}"])
                            S.op("dve", lambda: nc.vector.tensor_tensor(out=s2, in0=P[b0 + 1][:, 0:ntok], in1=s2, op=ALU.mult),
                                 reads=[f"P{b0 + 1}", f"scr{sb_}"], writes=[f"scr{sb_}"])
                            S.op("dve", lambda: nc.vector.tensor_tensor(out=mT[:, c, tk], in0=s1, in1=s2, op=ALU.add),
                                 reads=[f"scr{sa_}", f"scr{sb_}"], writes=["mT"])
                add([(0, 8, 256, w_co_v[:, :, cp * 256:(cp + 1) * 256]),
                     (2048, 8, 256, w_in_v[:, :, OGR + cp * 256: OGR + (cp + 1) * 256]),
                     (4096, 8, 256, w_in_v[:, :, OGC + cp * 256: OGC + (cp + 1) * 256])], ph_B2)

            def ln_a1(src_res, src, col):
                sbn = stat[:, 0:12].rearrange("p (a b) -> p a b", a=2)
                c0 = 12 + 4 * (col % 8)
                for a in range(2):
                    S.op("dve", lambda a=a: nc.vector.bn_stats(out=sbn[:, a, :], in_=src[:, a * 512:(a + 1) * 512]),
                         reads=[src_res], writes=[f"bn{a}"])
                S.op("dve", lambda: nc.vector.bn_aggr(out=stat[:, c0:c0 + 2], in_=stat[:, 0:12]),
                     reads=["bn0", "bn1"], writes=[f"mv{c0}"])
                S.op("dve", lambda: nc.vector.tensor_scalar(out=stat[:, c0 + 2:c0 + 3], in0=stat[:, c0 + 1:c0 + 2],
                                                            scalar1=LN_EPS, scalar2=None, op0=ALU.add),
                     reads=[f"mv{c0}"], writes=[f"rstd{c0}"])
                S.op("pool", lambda: nc.gpsimd.tensor_tensor(out=stat[:, c0 + 2:c0 + 3], in0=stat[:, c0 + 2:c0 + 3], in1=mhalf[:], op=ALU.pow),
                     reads=[f"rstd{c0}", "mhalf"], writes=[f"rstd{c0}"])

            def ln_a2(src_res, src, dst_res, dst, col):
                c0 = 12 + 4 * (col % 8)
                S.op("dve", lambda: nc.vector.scalar_tensor_tensor(out=stat[:, c0 + 3:c0 + 4], in0=stat[:, c0:c0 + 1], scalar=-1.0,
                                                                    in1=stat[:, c0 + 2:c0 + 3], op0=ALU.mult, op1=ALU.mult),
                     reads=[f"mv{c0}", f"rstd{c0}"], writes=[f"nmr{c0}"])
                S.op("act", lambda: nc.scalar.activation(out=dst, in_=src, func=AF.Identity,
                                                         scale=stat[:, c0 + 2:c0 + 3], bias=stat[:, c0 + 3:c0 + 4]),
                     reads=[src_res, f"rstd{c0}", f"nmr{c0}"], writes=[dst_res])

            def ln_b(dst_res, dst):
                S.op("dve", lambda: nc.vector.tensor_tensor(out=dst, in0=dst, in1=lnp[0][:], op=ALU.mult),
                     reads=[dst_res, "lnp"], writes=[dst_res])
                S.op("dve", lambda: nc.vector.tensor_tensor(out=dst, in0=dst, in1=lnp[1][:], op=ALU.add),
                     reads=[dst_res, "lnp"], writes=[dst_res])

            def ph_O1(wb):
                hold["wo0"] = wb[:, 0:4096].rearrange("p (k n) -> p k n", k=8)
                S.dma("sp", "lnp", [(lnp[0][:], lnp_d[0]), (lnp[1][:], lnp_d[1])], writes=["lnp"])
            add([(0, 8, 512, w_o_v[:, :, 0:512])], ph_O1)

            def ph_O2(wb):
                wo = [hold["wo0"], wb[:, 0:4096].rearrange("p (k n) -> p k n", k=8)]

                def tr(t):
                    sl = t % 2
                    tt = slice(t * 128, (t + 1) * 128)
                    for hf in range(2):
                        bk = 4 + 2 * sl + hf

                        def f(hf=hf, bk=bk, t=t):
                            for k4 in range(4):
                                kc = hf * 4 + k4
                                ins = nc.tensor.transpose(P[bk][:, k4 * 128:(k4 + 1) * 128], hh[:, t, kc * 128:(kc + 1) * 128], idf[:])
                            return ins
                        S.op("pe", f, reads=[f"hh{t}", "idf"], writes=[f"P{bk}"])
                        for k4 in range(4):
                            kc = hf * 4 + k4
                            if hf == 0:
                                S.op("act", lambda kc=kc, k4=k4, bk=bk, tt=tt: nc.scalar.activation(
                                    out=hT[:, kc, tt], in_=P[bk][:, k4 * 128:(k4 + 1) * 128], func=AF.Identity,
                                    scale=wcs[:, 3, kc:kc + 1], bias=wcs[:, 4, kc:kc + 1]),
                                    reads=[f"P{bk}", "wcs"], writes=[f"hT{t}"])
                            else:
                                S.op("dve", lambda kc=kc, k4=k4, bk=bk, tt=tt: nc.vector.tensor_scalar(
                                    out=hT[:, kc, tt], in0=P[bk][:, k4 * 128:(k4 + 1) * 128],
                                    scalar1=wcs[:, 3, kc:kc + 1], scalar2=wcs[:, 4, kc:kc + 1], op0=ALU.mult, op1=ALU.add),
                                    reads=[f"P{bk}", "wcs"], writes=[f"hT{t}"])

                for t, tl in enumerate(tiles):
                    sl = t % 2
                    tt = slice(t * 128, (t + 1) * 128)
                    S.dma("sp", f"xs{sl}", [(xsb[sl][:], tl["x"])], writes=[f"xs{sl}"])
                    if t > 0:
                        ln_a2(f"hh{t - 1}", hh[:, t - 1, :], f"hh{t - 1}", hh[:, t - 1, :], t - 1)
                    for nh in range(2):
                        bk = 2 * sl + nh

                        def f(nh=nh, bk=bk, tt=tt):
                            for kc in range(8):
                                ins = nc.tensor.matmul(P[bk][:, :], mT[:, kc, tt], wo[nh][:, kc, :], start=(kc == 0), stop=(kc == 7))
                            return ins
                        S.op("pe", f, reads=["W", "mT"], writes=[f"P{bk}"])
                        S.op("dve", lambda nh=nh, bk=bk, t=t, sl=sl: nc.vector.scalar_tensor_tensor(
                            out=hh[:, t, nh * 512:(nh + 1) * 512], in0=xsb[sl][:, nh * 512:(nh + 1) * 512], scalar=ALPHA,
                            in1=P[bk][:, :], op0=ALU.mult, op1=ALU.add),
                            reads=[f"P{bk}", f"xs{sl}"], writes=[f"hh{t}"])
                    ln_a1(f"hh{t}", hh[:, t, :], t)
                    if t > 1:
                        tr(t - 2)
                ln_a2(f"hh{nt - 1}", hh[:, nt - 1, :], f"hh{nt - 1}", hh[:, nt - 1, :], nt - 1)
                tr(max(0, nt - 2))
                deferred.append(lambda: tr(nt - 1))
            add([(0, 8, 512, w_o_v[:, :, 512:1024])], ph_O2)

            for jp in range(NJ // 2):
                def ph_F1(wb, jp=jp):
                    it = 0
                    for ji in range(2):
                        j = 2 * jp + ji
                        for (tok0, ntok, sg) in chunks:
                            tk = slice(tok0, tok0 + ntok)
                            b0 = 2 * (it % 4)
                            it += 1
                            for ab in range(2):
                                def f(ab=ab, tk=tk, b0=b0, ji=ji, ntok=ntok):
                                    for kc in range(8):
                                        ins = nc.tensor.matmul(P[b0 + ab][:, 0:ntok], wblk(wb, ab)[:, kc, ji * 128: ji * 128 + 128],
                                                               hT[:, kc, tk], start=(kc == 0), stop=(kc == 7))
                                    return ins
                                S.op("pe", f, reads=["W"] + [f"hT{t_}" for t_ in range(tok0 // 128, (tok0 + ntok) // 128)], writes=[f"P{b0 + ab}"])
                            if jp == 0 and it == 1:
                                run_deferred()
                            si = (b0 // 2) % 4
                            sa = scr[si][:, 0:ntok]
                            S.op("act", lambda: nc.scalar.activation(out=sa, in_=P[b0][:, 0:ntok], func=AF.Silu),
                                 reads=[f"P{b0}"], writes=[f"scr{si}"])
                            S.op("dve", lambda: nc.vector.tensor_tensor(out=zT[:, j, tk], in0=P[b0 + 1][:, 0:ntok], in1=sa, op=ALU.mult),
                                 reads=[f"P{b0 + 1}", f"scr{si}"], writes=["zT"])
                add([(0, 8, 256, w_up_v[:, :, jp * 256:(jp + 1) * 256]),
                     (2048, 8, 256, w_up_v[:, :, D_FF + jp * 256: D_FF + (jp + 1) * 256])], ph_F1)

            def f2_mm(w, t, bk):
                tt = slice(t * 128, (t + 1) * 128)

                def f():
                    for j in range(NJ):
                        ins = nc.tensor.matmul(P[bk][:, 0:256], zT[:, j, tt], w[:, j, :], start=(j == 0), stop=(j == NJ - 1))
                    return ins
                S.op("pe", f, reads=["W", "zT"], writes=[f"P{bk}"])

            for q in range(2):
                def ph_F2a(wb, q=q):
                    w = wb[:, 0:5632].rearrange("p (k n) -> p k n", k=NJ)
                    for t in range(nt):
                        bk = t % 8
                        f2_mm(w, t, bk)
                        for qq, with_psum in ((q, True), (q + 2, False)):
                            hs = hh[:, t, qq * 256:(qq + 1) * 256]
                            qs = slice(qq * 256, (qq + 1) * 256)
                            si = (2 * t + (0 if with_psum else 1)) % 4
                            tmp = scr[si][:, 0:256]
                            S.op("dve", lambda hs=hs, tmp=tmp, qs=qs: nc.vector.scalar_tensor_tensor(
                                out=tmp, in0=hs, scalar=ALPHA, in1=lnp[0][:, qs], op0=ALU.mult, op1=ALU.mult),
                                reads=[f"hh{t}", "lnp"], writes=[f"scr{si}"])
                            if with_psum:
                                S.op("dve", lambda tmp=tmp, qs=qs: nc.vector.scalar_tensor_tensor(
                                    out=tmp, in0=lnp[1][:, qs], scalar=ALPHA, in1=tmp, op0=ALU.mult, op1=ALU.add),
                                    reads=[f"scr{si}", "lnp"], writes=[f"scr{si}"])
                                S.op("dve", lambda hs=hs, tmp=tmp, bk=bk: nc.vector.tensor_tensor(out=hs, in0=tmp, in1=P[bk][:, 0:256], op=ALU.add),
                                     reads=[f"P{bk}", f"scr{si}"], writes=[f"hh{t}"])
                            else:
                                S.op("dve", lambda hs=hs, tmp=tmp, qs=qs: nc.vector.scalar_tensor_tensor(
                                    out=hs, in0=lnp[1][:, qs], scalar=ALPHA, in1=tmp, op0=ALU.mult, op1=ALU.add),
                                    reads=[f"scr{si}", "lnp"], writes=[f"hh{t}"])
                add([(0, NJ, 256, w_dn_v[:, :, q * 256:(q + 1) * 256])], ph_F2a)

            def ph_F2b(wb):
                hold["wd2"] = wb[:, 0:5632].rearrange("p (k n) -> p k n", k=NJ)
            add([(0, NJ, 256, w_dn_v[:, :, 512:768])], ph_F2b)

            def ph_F2c(wb):
                ws = [hold["wd2"], wb[:, 0:5632].rearrange("p (k n) -> p k n", k=NJ)]
                S.dma("sp", "lnp", [(lnp[0][:], lnp_d[2]), (lnp[1][:], lnp_d[3])], writes=["lnp"])

                R3f = R3[:].bitcast(F32)

                def stg(t):
                    p_ = t % 2
                    return R3f[:, p_ * 1024:(p_ + 1) * 1024], [f"stg{p_}"]

                def fin(t):
                    dst, names = stg(t)
                    S.op("dve", lambda: nc.vector.tensor_tensor(out=dst, in0=dst, in1=lnp[0][:], op=ALU.mult),
                         reads=names + ["lnp"], writes=names)
                    S.op("dve", lambda: nc.vector.tensor_tensor(out=dst, in0=dst, in1=lnp[1][:], op=ALU.add),
                         reads=names + ["lnp"], writes=names)
                    S.dma("act", f"yo{t % 2}", [(tiles[t]["y"], dst)], reads=names + [f"hT{t_}" for t_ in range(nt)] + ["cvT"])
                for t in range(nt):
                    sl = t % 2
                    for i_, qq in enumerate((2, 3)):
                        bk = (2 * t + i_) % 8
                        f2_mm(ws[i_], t, bk)
                        hs = hh[:, t, qq * 256:(qq + 1) * 256]
                        S.op("dve", lambda hs=hs, bk=bk: nc.vector.tensor_tensor(out=hs, in0=hs, in1=P[bk][:, 0:256], op=ALU.add),
                             reads=[f"P{bk}", f"hh{t}"], writes=[f"hh{t}"])
                    ln_a1(f"hh{t}", hh[:, t, :], t)
                    if t > 0:
                        fin(t - 1)
                    dst, names = stg(t)
                    c0 = 12 + 4 * (t % 8)
                    S.op("dve", lambda: nc.vector.scalar_tensor_tensor(out=stat[:, c0 + 3:c0 + 4], in0=stat[:, c0:c0 + 1], scalar=-1.0,
                                                                        in1=stat[:, c0 + 2:c0 + 3], op0=ALU.mult, op1=ALU.mult),
                         reads=[f"mv{c0}", f"rstd{c0}"], writes=[f"nmr{c0}"])
                    S.op("act", lambda: nc.scalar.activation(out=dst, in_=hh[:, t, :], func=AF.Identity,
                                                             scale=stat[:, c0 + 2:c0 + 3], bias=stat[:, c0 + 3:c0 + 4]),
                         reads=[f"hh{t}", f"rstd{c0}", f"nmr{c0}"], writes=names)
                fin(nt - 1)
            add([(0, NJ, 256, w_dn_v[:, :, 768:1024])], ph_F2c)

        def ptile(r0):
            return dict(kind=0, x=xp[r0:r0 + 128, :], y=yp[r0:r0 + 128, :])

        def stile(i):
            return dict(kind=1, x=xsm[i * 128:(i + 1) * 128, :], y=ys[i * 128:(i + 1) * 128, :], seqs=(2 * i, 2 * i + 1))
        NS = NSAMP * DEC_SEQ
        emit_st([stile(0), stile(1)] + [ptile(128 * i) for i in range(4)],
                [(0, 256, 0), (256, 512, 1)],
                [dict(kind=1, tok0=0, L=DEC_SEQ, nseq=NSAMP, ubase=0), dict(kind=0, tok0=256, L=512, nseq=1, ubase=NSAMP * (DEC_SEQ + 2))],
                0, True, False)
        emit_st([ptile(512 + 128 * i) for i in range(6)], [(0, 384, 0), (384, 384, 0)],
                [dict(kind=0, tok0=0, L=768, nseq=1, ubase=0)], NS + 512, False, False)
        emit_st([ptile(1280 + 128 * i) for i in range(6)], [(0, 384, 0), (384, 384, 0)],
                [dict(kind=0, tok0=0, L=768, nseq=1, ubase=0)], NS + 1280, False, True)

        widx = [i for i, (l, _) in enumerate(steps) if l is not None]
        slot_of = {si: k % NWB for k, si in enumerate(widx)}
        emitted = 0

        def emit_load(k):
            si = widx[k]
            sl = slot_of[si]
            pairs = []
            for (off, kk, ncol, src) in steps[si][0]:
                dst = Wb[sl][:, off:off + kk * ncol].rearrange("p (k n) -> p k n", k=kk)
                pairs.append((dst, src))
            S.dma("pool", f"W{sl}", pairs, writes=[f"Wb{sl}"])

        class _Alias:
            cur = []
        orig_sync, orig_record = S._sync, S._record

        def _map(names):
            out = []
            for n in names:
                if n == "W":
                    out.extend(_Alias.cur)
                else:
                    out.append(n)
            return out
        S._sync = lambda eng, r, w: orig_sync(eng, _map(r), _map(w))
        S._record = lambda tok, r, w: orig_record(tok, _map(r), _map(w))

        prev_w = None
        for si, (loads, fn) in enumerate(steps):
            if loads is not None:
                k = widx.index(si)
                while emitted < min(len(widx), k + NWB - 1):
                    emit_load(emitted)
                    emitted += 1
                sl = slot_of[si]
                _Alias.cur = [f"Wb{sl}"] + ([f"Wb{prev_w}"] if prev_w is not None else [])
                fn(Wb[sl])
                prev_w = sl
            else:
                _Alias.cur = []
                fn(None)
            if stop_after is not None and si == stop_after:
                S.barrier()
                d1 = nc.dram_tensor("dbg_R1", [128, R1N], BF16, kind="ExternalOutput").ap()
                d2 = nc.dram_tensor("dbg_R2", [128, NTL * 1024], F32, kind="ExternalOutput").ap()
                d3 = nc.dram_tensor("dbg_R3", [128, 8 * TS], BF16, kind="ExternalOutput").ap()
                S.dma("sp", "dbg", [(d1[:, :], R1[:]), (d2[:, :], R2[:]), (d3[:, :], R3[:])])
                break
        S.finish()
    return nc


_CACHE = {}


def kernel(x_prompt, x_sample, state_ret, cache_conv, w_in, w_conv, w_ret_out, w_conv_out,
           w_o, ln1_g, ln1_b, w_up, w_down, ln2_g, ln2_b):
    f32 = np.float32
    consts, cdec = _constants()
    if "nc" not in _CACHE:
        _CACHE["nc"] = build_program(cdec)
    nc = _CACHE["nc"]

    A = lambda a: np.ascontiguousarray(np.asarray(a, dtype=f32))
    x_prompt, x_sample = A(x_prompt), A(x_sample)
    state_ret, cache_conv = A(state_ret), A(cache_conv)
    wc = A(np.concatenate([np.asarray(w_conv)[0].reshape(3, 8, 128).transpose(2, 0, 1).reshape(128, 24),
                           np.asarray(ln1_g)[0].reshape(8, 128).T, np.asarray(ln1_b)[0].reshape(8, 128).T], axis=1))
    lnp = A(np.stack([np.broadcast_to(np.asarray(v)[0][None, :], (128, D)) for v in (ln1_g, ln1_b, ln2_g, ln2_b)]))
    shared = dict(w_in=A(w_in)[0], w_ro=A(w_ret_out)[0], w_co=A(w_conv_out)[0], w_o=A(w_o)[0],
                  w_up=A(w_up)[0], w_dn=A(w_down)[0], wc=wc, lnp=lnp, **consts)
    in_maps = []
    for c in range(NCORES):
        m = dict(shared)
        m["xp"] = x_prompt[c]
        m["xs"] = x_sample[NSAMP * c:NSAMP * (c + 1)].reshape(NSAMP * DEC_SEQ, D)
        m["sret"] = state_ret[0, NSAMP * c:NSAMP * (c + 1)]
        m["cconv"] = cache_conv[0, NSAMP * c:NSAMP * (c + 1)]
        in_maps.append(m)
    res = run_bass_kernel_spmd(nc, in_maps, core_ids=list(range(NCORES)))
    R = res.results
    y_p = np.stack([R[c]["yp"] for c in range(NCORES)]).astype(f32)
    y_s = np.concatenate([R[c]["ys"].reshape(NSAMP, DEC_SEQ, D) for c in range(NCORES)]).astype(f32)
    s_p = np.stack([R[c]["sp_o"] for c in range(NCORES)])[None].astype(f32)
    c_p = np.stack([R[c]["cp_o"] for c in range(NCORES)])[None].astype(f32)
    s_s = np.concatenate([R[c]["ss_o"] for c in range(NCORES)])[None].astype(f32)
    c_s = np.concatenate([R[c]["cs_o"] for c in range(NCORES)])[None].astype(f32)
    return (y_p, y_s, s_p, c_p, s_s, c_s)
```

```python
import math
from contextlib import ExitStack

import numpy as np
import concourse.bass as bass
import concourse.mybir as mybir
from concourse.bass_utils import run_bass_kernel_spmd

F32 = mybir.dt.float32
BF16 = mybir.dt.bfloat16
AF = mybir.ActivationFunctionType
ALU = mybir.AluOpType

D = 1024
SEQ = 2048
DEC_SEQ = 64
NSAMP = 4
H = 4
DK = 256
DV = 512
D_IN = 11264
D_FF = 2816
NJ = D_FF // 128
LN_EPS = 1e-5
RMS_EPS = 1e-6
ALPHA = 2.0 ** 0.25
PAST = 2048
NCORES = 8

OQ, OK_, OV, OG, OCB, OCC, OCX, OGR, OGC = 0, 1024, 2048, 4096, 6144, 7168, 8192, 9216, 10240


class Sched:
    def __init__(self, nc, es):
        self.nc = nc
        self.es = es
        self.E = {"pe": nc.tensor, "act": nc.scalar, "dve": nc.vector, "pool": nc.gpsimd, "sp": nc.sync}
        self.sem = {k: es.enter_context(nc.semaphore("sem_" + k)) for k in ("pe", "act", "dve", "pool")}
        self.cnt = {k: 0 for k in self.sem}
        self.dsem = {}
        self.dcnt = {}
        self.seen = {k: {} for k in self.E}
        self.lw = {}
        self.rd = {}

    def _wait(self, eng, tok):
        name, sem, val = tok
        if self.seen[eng].get(name, 0) >= val:
            return
        self.E[eng].wait_ge(sem, val)
        self.seen[eng][name] = val

    def _sync(self, eng, reads, writes):
        toks = []
        for r in reads:
            toks.extend(self.lw.get(r, {}).values())
        for w in writes:
            for t in self.lw.get(w, {}).values():
                if t[0] != eng or (eng != "pe" and w in ("junk",)):
                    toks.append(t)
            for t in self.rd.get(w, {}).values():
                if t[0] != eng or eng != "pe":
                    toks.append(t)
        for t in toks:
            self._wait(eng, t)

    def _record(self, tok, reads, writes):
        for r in reads:
            self.rd.setdefault(r, {})[tok[0]] = tok
        for w in writes:
            self.lw.setdefault(w, {})[tok[0]] = tok
            self.rd[w] = {}

    def op(self, eng, fn, reads=(), writes=()):
        self._sync(eng, reads, writes)
        inst = fn()
        self.cnt[eng] += 1
        inst.then_inc(self.sem[eng], 1)
        tok = (eng, self.sem[eng], self.cnt[eng])
        self._record(tok, reads, writes)
        return tok

    def dma(self, q, key, pairs, reads=(), writes=()):
        if key not in self.dsem:
            self.dsem[key] = self.es.enter_context(self.nc.semaphore("dsem_" + key))
            self.dcnt[key] = 0
        self._sync(q, reads, writes)
        for (o, i) in pairs:
            self.E[q].dma_start(out=o, in_=i).then_inc(self.dsem[key], 16)
            self.dcnt[key] += 16
        tok = ("d_" + key, self.dsem[key], self.dcnt[key])
        self._record(tok, reads, writes)
        return tok

    def barrier(self):
        for e in self.E:
            for k in self.sem:
                if k != e and self.cnt[k] > 0:
                    self._wait(e, (k, self.sem[k], self.cnt[k]))
            for k in self.dsem:
                if self.dcnt[k] > 0:
                    self._wait(e, ("d_" + k, self.dsem[k], self.dcnt[k]))

    def finish(self):
        for k in self.sem:
            if self.cnt[k] > 0:
                self._wait("sp", (k, self.sem[k], self.cnt[k]))
        for k in self.dsem:
            if self.dcnt[k] > 0:
                self._wait("sp", ("d_" + k, self.dsem[k], self.dcnt[k]))


def _constants():
    f32 = np.float32
    half = DK // 2
    inv = 1.0 / (10000.0 ** np.linspace(0.0, 1.0, half, dtype=np.float64))
    pos = np.concatenate([np.tile(PAST + np.arange(DEC_SEQ), NSAMP), np.arange(SEQ)]).astype(np.float64)
    ang = pos[None, :] * inv[:, None]
    cs = np.stack([np.cos(ang), np.sin(ang)], axis=1).astype(f32)

    lg = np.log(1.0 - 2.0 ** (-5.0 - np.arange(H, dtype=np.float64)))
    i = np.arange(128)
    decq = np.zeros((3, H, 128), np.float64)
    updk = np.zeros((128, 3, H), np.float64)
    for h in range(H):
        decq[0, h] = np.exp((i + 1) * lg[h])
        decq[1, h, :64] = np.exp((i[:64] + 1) * lg[h])
        decq[2, h, 64:] = np.exp((i[64:] - 64 + 1) * lg[h])
        updk[:, 0, h] = np.exp((127 - i) * lg[h]) / 16.0
        updk[:64, 1, h] = np.exp((63 - i[:64]) * lg[h]) / 16.0
        updk[64:, 2, h] = np.exp((127 - i[64:]) * lg[h]) / 16.0
    maskT = np.zeros((128, 2, H, 128), np.float64)
    jj, ii = np.meshgrid(i, i, indexing="ij")
    same = (jj // 64) == (ii // 64)
    for h in range(H):
        dec_p = np.exp((ii + 1) * lg[h])
        dec_s = np.exp(((ii % 64) + 1) * lg[h])
        intra = np.exp(np.abs(ii - jj) * lg[h])
        cross = np.exp((ii - jj) * lg[h])
        mp = np.where(same, intra, np.where(ii > jj, cross, 0.0))
        maskT[:, 0, h, :] = mp / dec_p / 16.0
        maskT[:, 1, h, :] = np.where(same, intra, 0.0) / dec_s / 16.0
    decq_b = np.broadcast_to(decq.reshape(1, 3 * H * 128), (128, 3 * H * 128))
    cdec = np.exp(np.array([128.0, 64.0])[:, None] * lg[None, :])
    return dict(
        cs=np.ascontiguousarray(cs),
        decq=np.ascontiguousarray(decq_b, dtype=f32),
        updk=np.ascontiguousarray(updk.reshape(128, 3 * H), dtype=f32),
        maskT=np.ascontiguousarray(maskT.reshape(128, 2 * H * 128), dtype=f32),
        ident=np.eye(128, dtype=f32),
    ), cdec


def build_program(cdec, stop_after=None):
    nc = bass.Bass("TRN2", target_bir_lowering=False)

    def din(name, shape):
        return nc.dram_tensor(name, list(shape), F32, kind="ExternalInput").ap()

    def dout(name, shape):
        return nc.dram_tensor(name, list(shape), F32, kind="ExternalOutput").ap()

    xp = din("xp", (SEQ, D))
    xsm = din("xs", (NSAMP * DEC_SEQ, D))
    sret = din("sret", (NSAMP, H, DK, DV))
    cconv = din("cconv", (NSAMP, 2, D))
    w_in = din("w_in", (D, D_IN))
    w_ro = din("w_ro", (2 * D, D))
    w_co = din("w_co", (D, D))
    w_o = din("w_o", (D, D))
    w_up = din("w_up", (D, 2 * D_FF))
    w_dn = din("w_dn", (D_FF, D))
    wc_d = din("wc", (128, 40))
    lnp_d = din("lnp", (4, 128, D))
    cs_d = din("cs", (128, 2, NSAMP * DEC_SEQ + SEQ))
    decq_d = din("decq", (128, 3 * H * 128))
    updk_d = din("updk", (128, 3 * H))
    mask_d = din("maskT", (128, 2 * H * 128))
    ident_d = din("ident", (128, 128))

    yp = dout("yp", (SEQ, D))
    ys = dout("ys", (NSAMP * DEC_SEQ, D))
    sp_o = dout("sp_o", (H, DK, DV))
    cp_o = dout("cp_o", (2, D))
    ss_o = dout("ss_o", (NSAMP, H, DK, DV))
    cs_o = dout("cs_o", (NSAMP, 2, D))

    w_in_v = w_in.rearrange("(kc p) n -> p kc n", p=128)
    w_ro_v = w_ro.rearrange("(kc p) n -> p kc n", p=128)
    w_co_v = w_co.rearrange("(kc p) n -> p kc n", p=128)
    w_o_v = w_o.rearrange("(kc p) n -> p kc n", p=128)
    w_up_v = w_up.rearrange("(kc p) n -> p kc n", p=128)
    w_dn_v = w_dn.rearrange("(kc p) n -> p kc n", p=128)

    TS = 768
    NTL = TS // 128
    es = ExitStack()
    with es:
        def sb(name, shape, dt):
            return es.enter_context(nc.sbuf_tensor("sb_" + name, list(shape), dt))

        R1N = 8 * TS + 2 * 1024 + 2 * TS + 8 * 256 + NTL * 512 * 2
        R1 = sb("R1", (128, R1N), BF16)
        R2 = sb("R2", (128, NTL * 1024), F32)
        R3 = sb("R3", (128, 8 * TS), BF16)
        St = sb("St", (128, 4, 2, 512), F32)
        Ss = sb("Ss", (128, 4, 2, 512), F32)
        Sbf = sb("Sbf", (128, 2, 2, 512), BF16)
        cst = sb("cst", (128, 2, TS), F32)
        xsb = [sb("xs0", (128, 1024), F32), sb("xs1", (128, 1024), F32)]
        NWB = 4
        Wb = [sb(f"W{i}", (128, 6144), BF16) for i in range(NWB)]
        lnp = [sb("lnp0", (128, 1024), F32), sb("lnp1", (128, 1024), F32)]
        UBN = 780
        ub = [sb("ub0", (128, UBN), F32), sb("ub1", (128, UBN), F32)]
        scrT = sb("scr", (128, 2048), F32)
        scr = [scrT[:, i * 512:(i + 1) * 512] for i in range(4)]
        maskt = sb("maskt", (128, 2, H, 128), F32)
        decq = sb("decq", (128, 3, H, 128), F32)
        updk = sb("updk", (128, 3, H), F32)
        wcs = sb("wcs", (128, 5, 8), F32)
        idf = sb("idf", (128, 128), F32)
        idb = sb("idb", (128, 128), BF16)
        ucar = sb("ucar", (128, 8, 2), F32)
        haloT = sb("haloT", (128, 8, 8), F32)
        scT = sb("scT", (128, 8, 8), F32)
        sTm = [sb("sTm0", (128, 128), BF16), sb("sTm1", (128, 128), BF16)]
        ogb = [sb("og0", (128, 512), BF16), sb("og1", (128, 512), BF16)]
        junk = sb("junk", (128, 512), BF16)
        stat = sb("stat", (128, 64), F32)
        mhalf = sb("mhalf", (128, 1), F32)
        P = [es.enter_context(nc.psum_tensor(f"P{i}", [128, 512], F32)) for i in range(8)]

        S = Sched(nc, es)

        def v3(t, off, a, b):
            return t[:, off:off + a * b].rearrange("p (a b) -> p a b", a=a)

        o_ = 0
        xT = v3(R1, o_, 8, TS); o_ += 8 * TS
        HB0 = o_
        qpT = v3(R1, o_, 2, 1024); o_ += 2048
        kT = v3(R1, o_, 2, TS); o_ += 2 * TS
        kp = v3(R1, o_, 8, 256); o_ += 2048
        vv = v3(R1, o_, NTL, 512); o_ += NTL * 512
        gg = v3(R1, o_, NTL, 512); o_ += NTL * 512
        assert o_ == R1N and R1N >= 22 * TS and R1N - HB0 >= 8 * TS
        mT = v3(R1, HB0, 8, TS)
        zT = v3(R1, 0, 22, TS)
        hh = v3(R2, 0, NTL, 1024)
        goT = R2[:].bitcast(BF16)[:, 0:16 * TS].rearrange("p (a b) -> p a b", a=16)
        cvT = v3(R3, 0, 8, TS)
        hT = v3(R3, 0, 8, TS)

        def wblk(wb, j):
            return wb[:, j * 2048:(j + 1) * 2048].rearrange("p (k n) -> p k n", k=8)

        S.dma("sp", "misc", [(maskt[:].rearrange("p a h i -> p (a h i)"), mask_d[:, :]),
                             (decq[:].rearrange("p a h i -> p (a h i)"), decq_d[:, :]),
                             (updk[:].rearrange("p a h -> p (a h)"), updk_d[:, :]),
                             (wcs[:].rearrange("p a c -> p (a c)"), wc_d[:, :]),
                             (idf[:], ident_d[:, :])],
              writes=["maskt", "decq", "updk", "wcs", "idf"])
        S.dma("pool", "misc2", [(idb[:], ident_d[:, :])], writes=["idb"])
        S.op("pool", lambda: nc.gpsimd.memset(mhalf[:], -0.5), writes=["mhalf"])

        hin = xsb[1][0:8, :]
        S.op("dve", lambda: nc.vector.memset(xsb[1][:], 0.0), writes=["xs1"])
        S.dma("sp", "xs1", [(hin, cconv.rearrange("s r d -> (s r) d"))], writes=["xs1"])

        for hf in range(2):
            def f_h(hf=hf):
                for k4 in range(4):
                    c = hf * 4 + k4
                    ins = nc.tensor.transpose(P[hf][:, k4 * 128:(k4 + 1) * 128], xsb[1][:, c * 128:(c + 1) * 128], idf[:])
                return ins
            S.op("pe", f_h, reads=["xs1", "idf"], writes=[f"P{hf}"])
            S.op("dve", lambda hf=hf: nc.vector.tensor_copy(out=haloT[:, hf * 4:(hf + 1) * 4, :],
                                                            in_=P[hf][:].rearrange("p (a b) -> p a b", a=4)[:, :, 0:8]),
                 reads=[f"P{hf}"], writes=["haloT"])

        def flush_rows(src_fn, nrows, dst):
            for hf in range(2):
                stg = scr[hf][:, 0:512].rearrange("p (a b) -> p a b", a=4)
                S.op("dve", lambda hf=hf, stg=stg: nc.vector.tensor_copy(out=stg[:, :, 0:nrows], in_=src_fn(hf * 4)),
                     reads=["ucar", "scT"], writes=[f"scr{hf}"])

                def f(hf=hf):
                    for k4 in range(4):
                        ins = nc.tensor.transpose(P[6 + hf][:, k4 * 128:(k4 + 1) * 128], scr[hf][:, k4 * 128:(k4 + 1) * 128], idf[:])
                    return ins
                S.op("pe", f, reads=[f"scr{hf}", "idf"], writes=[f"P{6 + hf}"])
                S.op("act", lambda hf=hf: nc.scalar.activation(out=hin[0:nrows, hf * 512:(hf + 1) * 512], in_=P[6 + hf][0:nrows, :], func=AF.Copy),
                     reads=[f"P{6 + hf}"], writes=["xs1"])
            S.dma("sp", "xs1", [(dst, hin[0:nrows, :])], reads=["xs1"])

        steps = []

        def add(loads, fn):
            steps.append((loads, fn))

        def emit_st(tiles, chunks, segs, col0, first, last):
            nt = len(tiles)
            T = nt * 128
            has_prompt = any(tl["kind"] == 0 for tl in tiles)
            has_sample = any(tl["kind"] == 1 for tl in tiles)
            for t, tl in enumerate(tiles):
                if tl["kind"] == 0:
                    tl["vars"] = [(0, t * 128, t, 0)]
                else:
                    tl["vars"] = [(1, t * 128, t, 0), (2, TS + t * 128, NTL + t, 1)]
            tchunk = {}
            for ci_, (tok0_, ntok_, _sg) in enumerate(chunks):
                for t_ in range(tok0_ // 128, (tok0_ + ntok_) // 128):
                    tchunk[t_] = ci_

            def xTn(tok0, ntok):
                return [f"xT{t_}" for t_ in range(tok0 // 128, (tok0 + ntok) // 128)]
            fpt = [t for t, tl in enumerate(tiles) if tl["kind"] == 0]
            first_prompt_tile = fpt[0] if fpt else None
            last_prompt_tile = fpt[-1] if fpt else None

            def ph_A0(_):
                S.dma("sp", "cs", [(cst[:, :, 0:T], cs_d[:, :, col0:col0 + T])], writes=["cst"])
                xslots = [(xsb[0][:], ["xs0"], "xs0"), (xsb[1][:], ["xs1"], "xs1"),
                          (scrT[:, 0:1024], ["scr0", "scr1"], "xq0"), (scrT[:, 1024:2048], ["scr2", "scr3"], "xq1")]
                for t, tl in enumerate(tiles):
                    xbuf, xnames, xkey = xslots[t % 4]
                    S.dma("sp", xkey, [(xbuf, tl["x"])], writes=xnames)
                    for hf in range(2):
                        bk = (2 * t + hf) % 8

                        def f(bk=bk, hf=hf, xbuf=xbuf):
                            for k4 in range(4):
                                kc = hf * 4 + k4
                                ins = nc.tensor.transpose(P[bk][:, k4 * 128:(k4 + 1) * 128],
                                                          xbuf[:, kc * 128:(kc + 1) * 128], idf[:])
                            return ins
                        S.op("pe", f, reads=xnames + ["idf"], writes=[f"P{bk}"])
                        src = P[bk][:].rearrange("p (a b) -> p a b", a=4)
                        dst = xT[:, hf * 4:hf * 4 + 4, t * 128:(t + 1) * 128]
                        S.op("act", lambda src=src, dst=dst: nc.scalar.activation(out=dst, in_=src, func=AF.Copy),
                             reads=[f"P{bk}"], writes=[f"xT{t}"])
            add(None, ph_A0)

            deferred = []

            def run_deferred():
                while deferred:
                    deferred.pop(0)()

            for h in range(H):
                def ph_QK(wb, h=h):
                    if has_sample:
                        seqs = sorted(s_ for tl in tiles if tl["kind"] == 1 for s_ in tl["seqs"])
                        S.dma("sp", "Sld", [(Ss[:, s_, :, :], sret[s_, h].rearrange("(two p) e -> p two e", p=128))
                                            for s_ in seqs], writes=["Ss"] + [f"Ss{s_}_{hf_}" for s_ in seqs for hf_ in range(2)])
                    if has_prompt and not first:
                        fs_ = first_prompt_tile % 2
                        S.op("act", lambda: nc.scalar.activation(out=Sbf[:, fs_, :, :], in_=St[:, h, :, :], func=AF.Copy),
                             reads=[f"St{h}_0", f"St{h}_1"], writes=[f"Sbf{fs_}"])
                    bi = 0
                    for (tok0, ntok, sg) in chunks:
                        tk = slice(tok0, tok0 + ntok)
                        ckind = segs[sg]["kind"]
                        for which in (1, 0):
                            b1, b2 = (0, 1) if bi % 2 == 0 else (2, 3)
                            bi += 1
                            for hf, bk in ((0, b1), (1, b2)):
                                def f(bk=bk, hf=hf, which=which, tk=tk, ntok=ntok):
                                    for kc in range(8):
                                        ins = nc.tensor.matmul(P[bk][:, 0:ntok],
                                                               wblk(wb, which)[:, kc, hf * 128: hf * 128 + 128],
                                                               xT[:, kc, tk], start=(kc == 0), stop=(kc == 7))
                                    return ins
                                S.op("pe", f, reads=["W"] + xTn(tok0, ntok), writes=[f"P{bk}"])
                            cos = cst[:, 0, tk]
                            sin = cst[:, 1, tk]
                            ra, rb, ra2, rb2 = (scr[i][:, 0:ntok] for i in range(4))
                            t1, t2 = P[b1][:, 0:ntok], P[b2][:, 0:ntok]
                            if which == 1:
                                o1, o2 = kT[:, 0, tk], kT[:, 1, tk]
                                w1, w2 = [f"kT{tchunk[tok0 // 128]}"], [f"kT{tchunk[tok0 // 128]}"]
                            else:
                                o1, o2 = ub[0][:, 0:ntok], ub[1][:, 0:ntok]
                                w1, w2 = ["ub0"], ["ub1"]
                            S.op("dve", lambda: nc.vector.tensor_tensor(out=ra, in0=t1, in1=cos, op=ALU.mult),
                                 reads=[f"P{b1}", "cst"], writes=["scr0"])
                            S.op("dve", lambda: nc.vector.tensor_tensor(out=rb, in0=t2, in1=sin, op=ALU.mult),
                                 reads=[f"P{b2}", "cst"], writes=["scr1"])
                            S.op("dve", lambda: nc.vector.tensor_tensor(out=ra2, in0=t1, in1=sin, op=ALU.mult),
                                 reads=[f"P{b1}", "cst"], writes=["scr2"])
                            S.op("dve", lambda: nc.vector.tensor_tensor(out=rb2, in0=t2, in1=cos, op=ALU.mult),
                                 reads=[f"P{b2}", "cst"], writes=["scr3"])
                            S.op("dve", lambda: nc.vector.tensor_tensor(out=o1, in0=ra, in1=rb, op=ALU.subtract),
                                 reads=["scr0", "scr1"], writes=w1)
                            S.op("dve", lambda: nc.vector.tensor_tensor(out=o2, in0=ra2, in1=rb2, op=ALU.add),
                                 reads=["scr2", "scr3"], writes=w2)
                            run_deferred()
                            if which == 0:
                                nct = ntok // 128
                                t0i = tok0 // 128
                                vlist = [(0, tok0)] if ckind == 0 else [(1, tok0), (2, TS + tok0)]
                                for (var, qcol) in vlist:
                                    for hf in range(2):
                                        src = ub[hf][:, 0:ntok].rearrange("p (a b) -> p a b", b=128)
                                        dq = decq[:, var, h, :].unsqueeze(1).broadcast_to([128, nct, 128])
                                        dst = qpT[:, hf, qcol:qcol + ntok].rearrange("p (a b) -> p a b", b=128)
                                        S.op("dve", lambda src=src, dq=dq, dst=dst: nc.vector.tensor_tensor(out=dst, in0=src, in1=dq, op=ALU.mult),
                                             reads=[f"ub{hf}", "decq"], writes=["qpT"])
                add([(0, 8, 256, w_in_v[:, :, OQ + h * 256: OQ + (h + 1) * 256]),
                     (2048, 8, 256, w_in_v[:, :, OK_ + h * 256: OK_ + (h + 1) * 256])], ph_QK)

                def ph_V(wb, h=h):
                    w = wb[:, 0:4096].rearrange("p (k n) -> p k n", k=8)
                    for t in range(nt):
                        bk = 1 + (t % 2)

                        def f(t=t, bk=bk):
                            for kc in range(8):
                                ins = nc.tensor.matmul(P[bk][:, :], xT[:, kc, t * 128:(t + 1) * 128], w[:, kc, :],
                                                       start=(kc == 0), stop=(kc == 7))
                            return ins
                        S.op("pe", f, reads=["W", f"xT{t}"], writes=[f"P{bk}"])
                        S.op("act", lambda t=t, bk=bk: nc.scalar.activation(out=vv[:, t, :], in_=P[bk][:, :], func=AF.Copy),
                             reads=[f"P{bk}"], writes=["vv"])
                add([(0, 8, 512, w_in_v[:, :, OV + h * 512: OV + (h + 1) * 512])], ph_V)

                def ph_G(wb, h=h):
                    w = wb[:, 0:4096].rearrange("p (k n) -> p k n", k=8)
                    for t in range(nt):
                        bk = 1 + (t % 2)

                        def f(t=t, bk=bk):
                            for kc in range(8):
                                ins = nc.tensor.matmul(P[bk][:, :], xT[:, kc, t * 128:(t + 1) * 128], w[:, kc, :],
                                                       start=(kc == 0), stop=(kc == 7))
                            return ins
                        S.op("pe", f, reads=["W", f"xT{t}"], writes=[f"P{bk}"])
                        S.op("act", lambda t=t, bk=bk: nc.scalar.activation(out=gg[:, t, :], in_=P[bk][:, :], func=AF.Silu),
                             reads=[f"P{bk}"], writes=["gg"])
                        tl = tiles[t]
                        bk = 4 + (t % 2)
                        pb = P[bk][:].bitcast(BF16)

                        def f(t=t, pb=pb):
                            for hf in range(2):
                                ins = nc.tensor.transpose(pb[:, hf * 128:(hf + 1) * 128], kT[:, hf, t * 128:(t + 1) * 128], idb[:])
                            return ins
                        S.op("pe", f, reads=[f"kT{tchunk[t]}", "idb"], writes=[f"P{bk}"])
                        for (var, _q, kslot, _vi) in tl["vars"]:
                            S.op("act", lambda var=var, kslot=kslot, pb=pb: nc.scalar.activation(
                                out=kp[:, kslot, :], in_=pb[:, 0:256], func=AF.Identity, scale=updk[:, var, h:h + 1]),
                                reads=[f"P{bk}", "updk"], writes=["kp"])
                    def post_og(t):
                        sl = t % 2
                        bo = 1 + sl
                        col = (h * 8 + t) % 32
                        S.op("dve", lambda: nc.vector.scalar_tensor_tensor(
                            out=ogb[sl][:], in0=P[bo][:, :], scalar=stat[:, 32 + col:33 + col], in1=gg[:, t, :],
                            op0=ALU.mult, op1=ALU.mult),
                            reads=[f"P{bo}", f"rs{col}", "gg"], writes=[f"og{sl}"])

                    def post_tr(t):
                        sl = t % 2
                        tt = slice(t * 128, (t + 1) * 128)
                        p7 = P[7][:].bitcast(BF16)[:, 0:512]

                        def f_t():
                            for ec in range(4):
                                ins = nc.tensor.transpose(p7[:, ec * 128:(ec + 1) * 128], ogb[sl][:, ec * 128:(ec + 1) * 128], idb[:])
                            return ins
                        S.op("pe", f_t, reads=[f"og{sl}", "idb"], writes=["P7"])
                        S.op("act", lambda: nc.scalar.activation(
                            out=goT[:, h * 4:(h + 1) * 4, tt], in_=p7.rearrange("p (a b) -> p a b", a=4), func=AF.Copy),
                            reads=["P7"], writes=["goT"])

                    def scores(t):
                        tl = tiles[t]
                        tt = slice(t * 128, (t + 1) * 128)
                        sl = t % 2
                        variants = tl["vars"]

                        def f_s():
                            n_mm = 2 * len(variants)
                            c = 0
                            for (var, qcol, _k, _vi) in variants:
                                for hf in range(2):
                                    ins = nc.tensor.matmul(P[0][:, 0:128], kT[:, hf, tt], qpT[:, hf, qcol:qcol + 128],
                                                           start=(c == 0), stop=(c == n_mm - 1))
                                    c += 1
                            return ins
                        S.op("pe", f_s, reads=[f"kT{tchunk[t]}", "qpT"], writes=["P0"])
                        S.op("dve", lambda: nc.vector.tensor_tensor(out=sTm[sl][:], in0=P[0][:, 0:128], in1=maskt[:, tl["kind"], h, :], op=ALU.mult),
                             reads=["P0", "maskt"], writes=[f"sTm{sl}"])

                    scores(0)
                    for t, tl in enumerate(tiles):
                        kind = tl["kind"]
                        variants = tl["vars"]
                        zero_state = (kind == 0 and first and t == first_prompt_tile)
                        sl = t % 2
                        if t + 1 < nt:
                            scores(t + 1)
                        if kind == 1:
                            for vi in range(2):
                                S.op("act", lambda vi=vi, tl=tl: nc.scalar.activation(out=Sbf[:, vi, :, :], in_=Ss[:, tl["seqs"][vi], :, :], func=AF.Copy),
                                     reads=[f"Ss{tl['seqs'][vi]}_0", f"Ss{tl['seqs'][vi]}_1"], writes=[f"Sbf{vi}"])
                        if t > 1:
                            post_og(t - 2)
                        upd = []
                        for (var, _q, kslot, vi) in variants:
                            for hf in range(2):
                                if kind == 0:
                                    bk = 3 + 2 * sl + hf
                                    sdst = St[:, h, hf, :]
                                    sres = f"St{h}_{hf}"
                                else:
                                    bk = 3 + 2 * vi + hf
                                    sdst = Ss[:, tl["seqs"][vi], hf, :]
                                    sres = f"Ss{tl['seqs'][vi]}_{hf}"

                                def f_u(kslot=kslot, t=t, hf=hf, bk=bk):
                                    return nc.tensor.matmul(P[bk][:, :], kp[:, kslot, hf * 128:(hf + 1) * 128], vv[:, t, :],
                                                            start=True, stop=True)
                                S.op("pe", f_u, reads=["kp", "vv"], writes=[f"P{bk}"])
                                upd.append((bk, sdst, sres, hf))
                        bo = 1 + sl

                        def f_o(t=t, sl=sl, bo=bo, zero_state=zero_state, variants=variants):
                            mm = [(sTm[sl][:], vv[:, t, :])]
                            if not zero_state:
                                for (var, qcol, _k, vi) in variants:
                                    for hf in range(2):
                                        mm.append((qpT[:, hf, qcol:qcol + 128], Sbf[:, (vi if tiles[t]["kind"] == 1 else t % 2), hf, :]))
                            for c, (l, r) in enumerate(mm):
                                ins = nc.tensor.matmul(P[bo][:, :], l, r, start=(c == 0), stop=(c == len(mm) - 1))
                            return ins
                        S.op("pe", f_o, reads=[f"sTm{sl}", "vv", "qpT"] + (["Sbf0", "Sbf1"] if kind == 1 else [f"Sbf{t % 2}"]), writes=[f"P{bo}"])
                        for (bk, sdst, sres, hf) in upd:
                            if zero_state:
                                S.op("dve", lambda sdst=sdst, bk=bk: nc.vector.tensor_copy(out=sdst, in_=P[bk][:, :]),
                                     reads=[f"P{bk}"], writes=[sres])
                            else:
                                g128 = float(cdec[kind, h])
                                S.op("dve", lambda sdst=sdst, bk=bk, g128=g128: nc.vector.scalar_tensor_tensor(
                                    out=sdst, in0=sdst, scalar=g128, in1=P[bk][:, :], op0=ALU.mult, op1=ALU.add),
                                    reads=[f"P{bk}", sres], writes=[sres])
                            if kind == 0 and t != last_prompt_tile:
                                ns_ = (t + 1) % 2
                                S.op("act", lambda sdst=sdst, hf=hf, ns_=ns_: nc.scalar.activation(out=Sbf[:, ns_, hf, :], in_=sdst, func=AF.Copy),
                                     reads=[sres], writes=[f"Sbf{ns_}"])
                        if kind == 1:
                            S.dma("sp", "Sst", [(ss_o[tl["seqs"][vi], h].rearrange("(two p) e -> p two e", p=128), Ss[:, tl["seqs"][vi], :, :])
                                                for vi in range(2)], reads=["Ss"] + [f"Ss{tl['seqs'][vi]}_{hf_}" for vi in range(2) for hf_ in range(2)])
                        elif last and t == last_prompt_tile:
                            S.dma("sp", "Sst", [(sp_o[h].rearrange("(two p) e -> p two e", p=128), St[:, h, :, :])],
                                  reads=[f"St{h}_0", f"St{h}_1"])
                        col = (h * 8 + t) % 32
                        S.op("act", lambda bo=bo, col=col: nc.scalar.activation(out=junk[:], in_=P[bo][:, :], func=AF.Square,
                                                                                 accum_out=stat[:, col:col + 1]),
                             reads=[f"P{bo}"], writes=["junk", f"ss{col}"])
                        S.op("pool", lambda col=col: nc.gpsimd.tensor_scalar(out=stat[:, 32 + col:33 + col], in0=stat[:, col:col + 1],
                                                                             scalar1=1.0 / DV, scalar2=RMS_EPS, op0=ALU.mult, op1=ALU.add),
                             reads=[f"ss{col}"], writes=[f"rs{col}"])
                        S.op("pool", lambda col=col: nc.gpsimd.tensor_tensor(out=stat[:, 32 + col:33 + col], in0=stat[:, 32 + col:33 + col],
                                                                             in1=mhalf[:], op=ALU.pow),
                             reads=[f"rs{col}", "mhalf"], writes=[f"rs{col}"])
                        if t > 1:
                            post_tr(t - 2)
                    for t_ in range(max(0, nt - 2), nt):
                        post_og(t_)
                        deferred.append(lambda t_=t_: post_tr(t_))
                add([(0, 8, 512, w_in_v[:, :, OG + h * 512: OG + (h + 1) * 512])], ph_G)

            for cp in range(4):
                def ph_C(wb, cp=cp):
                    for ci in range(2):
                        c = 2 * cp + ci
                        u = ub[c % 2]
                        ur = f"ub{c % 2}"
                        for sgd in segs:
                            if sgd["kind"] == 1:
                                u3 = u[:, sgd["ubase"]:sgd["ubase"] + sgd["nseq"] * (sgd["L"] + 2)].rearrange("p (s l) -> p s l", s=sgd["nseq"])
                                S.op("dve", lambda u3=u3, c=c, sgd=sgd: nc.vector.tensor_copy(
                                    out=u3[:, :, 0:2], in_=haloT[:, c, :].rearrange("p (s r) -> p s r", s=sgd["nseq"])),
                                    reads=["haloT"], writes=[ur])
                            elif first:
                                S.op("dve", lambda u=u, sgd=sgd: nc.vector.memset(u[:, sgd["ubase"]:sgd["ubase"] + 2], 0.0), writes=[ur])
                            else:
                                S.op("dve", lambda u=u, sgd=sgd, c=c: nc.vector.tensor_copy(out=u[:, sgd["ubase"]:sgd["ubase"] + 2], in_=ucar[:, c, :]),
                                     reads=["ucar"], writes=[ur])
                        for ic, (tok0, ntok, sg) in enumerate(chunks):
                            sgd = segs[sg]
                            tk = slice(tok0, tok0 + ntok)
                            b0 = 0 if (ci * len(chunks) + ic) % 2 == 0 else 3
                            for j in (2, 1, 0):
                                def f(j=j, tk=tk, bk=b0 + j, ci=ci, ntok=ntok):
                                    for kc in range(8):
                                        ins = nc.tensor.matmul(P[bk][:, 0:ntok], wblk(wb, j)[:, kc, ci * 128: ci * 128 + 128],
                                                               xT[:, kc, tk], start=(kc == 0), stop=(kc == 7))
                                    return ins
                                S.op("pe", f, reads=["W"] + xTn(tok0, ntok), writes=[f"P{b0 + j}"])
                            run_deferred()
                            cxs = scr[0][:, 0:ntok]
                            acc = scr[1][:, 0:ntok]
                            S.op("act", lambda cxs=cxs, b0=b0, ntok=ntok: nc.scalar.activation(out=cxs, in_=P[b0 + 2][:, 0:ntok], func=AF.Copy),
                                 reads=[f"P{b0 + 2}"], writes=["scr0"])
                            if sgd["kind"] == 0:
                                lt0 = tok0 - sgd["tok0"]

                                def uview(sh, u=u, sgd=sgd, lt0=lt0, ntok=ntok):
                                    return u[:, sgd["ubase"] + lt0 + sh: sgd["ubase"] + lt0 + sh + ntok]
                                pv = lambda a_: a_
                            else:
                                assert tok0 == sgd["tok0"] and ntok == sgd["nseq"] * sgd["L"]
                                u3 = u[:, sgd["ubase"]:sgd["ubase"] + sgd["nseq"] * (sgd["L"] + 2)].rearrange("p (s l) -> p s l", s=sgd["nseq"])

                                def uview(sh, u3=u3, sgd=sgd):
                                    return u3[:, :, sh:sh + sgd["L"]]
                                pv = lambda a_, sgd=sgd: a_.rearrange("p (s l) -> p s l", s=sgd["nseq"])
                            S.op("dve", lambda: nc.vector.tensor_tensor(
                                out=uview(2), in0=pv(P[b0 + 1][:, 0:ntok]), in1=pv(cxs), op=ALU.mult),
                                reads=[f"P{b0 + 1}", "scr0"], writes=[ur])
                            S.op("dve", lambda: nc.vector.tensor_scalar(
                                out=pv(acc), in0=uview(2), scalar1=wcs[:, 2, c:c + 1], scalar2=None, op0=ALU.mult),
                                reads=[ur, "wcs"], writes=["scr1"])
                            for tap in (1, 0):
                                S.op("dve", lambda tap=tap: nc.vector.scalar_tensor_tensor(
                                    out=pv(acc), in0=uview(tap), scalar=wcs[:, tap, c:c + 1], in1=pv(acc), op0=ALU.mult, op1=ALU.add),
                                    reads=[ur, "wcs", "scr1"], writes=["scr1"])
                            S.op("dve", lambda: nc.vector.tensor_tensor(
                                out=cvT[:, c, tk], in0=P[b0][:, 0:ntok], in1=acc, op=ALU.mult),
                                reads=[f"P{b0}", "scr1"], writes=["cvT"])
                        for sgd in segs:
                            if sgd["kind"] == 0:
                                e0 = sgd["ubase"] + sgd["L"]
                                S.op("act", lambda u=u, e0=e0, c=c: nc.scalar.activation(out=ucar[:, c, :], in_=u[:, e0:e0 + 2], func=AF.Copy),
                                     reads=[ur], writes=["ucar"])
                            else:
                                u3 = u[:, sgd["ubase"]:sgd["ubase"] + sgd["nseq"] * (sgd["L"] + 2)].rearrange("p (s l) -> p s l", s=sgd["nseq"])
                                S.op("act", lambda u3=u3, c=c, sgd=sgd: nc.scalar.activation(
                                    out=scT[:, c, :].rearrange("p (s r) -> p s r", s=sgd["nseq"]), in_=u3[:, :, sgd["L"]:sgd["L"] + 2], func=AF.Copy),
                                    reads=[ur], writes=["scT"])
                    if cp == 3 and has_sample:
                        flush_rows(lambda c0: scT[:, c0:c0 + 4, :], 8, cs_o.rearrange("s r d -> (s r) d"))
                    if cp == 3 and last:
                        flush_rows(lambda c0: ucar[:, c0:c0 + 4, :], 2, cp_o[:, :])
                add([(0, 8, 256, w_in_v[:, :, OCB + cp * 256: OCB + (cp + 1) * 256]),
                     (2048, 8, 256, w_in_v[:, :, OCC + cp * 256: OCC + (cp + 1) * 256]),
                     (4096, 8, 256, w_in_v[:, :, OCX + cp * 256: OCX + (cp + 1) * 256])], ph_C)

            hold = {}
            for cp in range(4):
                def ph_B1(wb, cp=cp):
                    hold["wa"] = wb[:, 0:4096].rearrange("p (k n) -> p k n", k=16)
                add([(0, 16, 256, w_ro_v[:, :, cp * 256:(cp + 1) * 256])], ph_B1)

                def ph_B2(wb, cp=cp):
                    wa = hold["wa"]
                    it = 0
                    for ci in range(2):
                        c = 2 * cp + ci
                        cc_ = slice(ci * 128, (ci + 1) * 128)
                        for (tok0, ntok, sg) in chunks:
                            tk = slice(tok0, tok0 + ntok)
                            b0 = 0 if it % 2 == 0 else 4
                            it += 1

                            def f_r(tk=tk, b0=b0, cc_=cc_, ntok=ntok):
                                for ec in range(16):
                                    ins = nc.tensor.matmul(P[b0][:, 0:ntok], wa[:, ec, cc_], goT[:, ec, tk], start=(ec == 0), stop=(ec == 15))
                                return ins
                            S.op("pe", f_r, reads=["W", "goT"], writes=[f"P{b0}"])
                            for j, src, rname in ((0, cvT, ["cvT"]), (1, xT, xTn(tok0, ntok)), (2, xT, xTn(tok0, ntok))):
                                def f(j=j, src=src, tk=tk, b0=b0, ci=ci, ntok=ntok):
                                    for kc in range(8):
                                        ins = nc.tensor.matmul(P[b0 + 1 + j][:, 0:ntok], wblk(wb, j)[:, kc, ci * 128: ci * 128 + 128],
                                                               src[:, kc, tk], start=(kc == 0), stop=(kc == 7))
                                    return ins
                                S.op("pe", f, reads=["W"] + rname, writes=[f"P{b0 + 1 + j}"])
                            sa_, sb_ = (0, 1) if b0 == 0 else (2, 3)
                            s1, s2 = scr[sa_][:, 0:ntok], scr[sb_][:, 0:ntok]
                            S.op("act", lambda: nc.scalar.activation(out=s1, in_=P[b0 + 2][:, 0:ntok], func=AF.Sigmoid),
                                 reads=[f"P{b0 + 2}"], writes=[f"scr{sa_}"])
                            S.op("act", lambda: nc.scalar.activation(out=s2, in_=P[b0 + 3][:, 0:ntok], func=AF.Sigmoid),
                                 reads=[f"P{b0 + 3}"], writes=[f"scr{sb_}"])
                            S.op("dve", lambda: nc.vector.tensor_tensor(out=s1, in0=P[b0][:, 0:ntok], in1=s1, op=ALU.mult),
                                 reads=[f"P{b0}", f"scr{sa_}"], writes=[f"scr{sa_}"])
                            S.op("dve", lambda: nc.vector.tensor_tensor(out=s2, in0=P[b0 + 1][:, 0:ntok], in1=s2, op=ALU.mult),
                                 reads=[f"P{b0 + 1}", f"scr{sb_}"], writes=[f"scr{sb_}"])
                            S.op("dve", lambda: nc.vector.tensor_tensor(out=mT[:, c, tk], in0=s1, in1=s2, op=ALU.add),
                                 reads=[f"scr{sa_}", f"scr{sb_}"], writes=["mT"])
                add([(0, 8, 256, w_co_v[:, :, cp * 256:(cp + 1) * 256]),
                     (2048, 8, 256, w_in_v[:, :, OGR + cp * 256: OGR + (cp + 1) * 256]),
                     (4096, 8, 256, w_in_v[:, :, OGC + cp * 256: OGC + (cp + 1) * 256])], ph_B2)

            def ln_a1(src_res, src, col):
                sbn = stat[:, 0:12].rearrange("p (a b) -> p a b", a=2)
                c0 = 12 + 4 * (col % 8)
                for a in range(2):
                    S.op("dve", lambda a=a: nc.vector.bn_stats(out=sbn[:, a, :], in_=src[:, a * 512:(a + 1) * 512]),
                         reads=[src_res], writes=[f"bn{a}"])
                S.op("dve", lambda: nc.vector.bn_aggr(out=stat[:, c0:c0 + 2], in_=stat[:, 0:12]),
                     reads=["bn0", "bn1"], writes=[f"mv{c0}"])
                S.op("dve", lambda: nc.vector.tensor_scalar(out=stat[:, c0 + 2:c0 + 3], in0=stat[:, c0 + 1:c0 + 2],
                                                            scalar1=LN_EPS, scalar2=None, op0=ALU.add),
                     reads=[f"mv{c0}"], writes=[f"rstd{c0}"])
                S.op("pool", lambda: nc.gpsimd.tensor_tensor(out=stat[:, c0 + 2:c0 + 3], in0=stat[:, c0 + 2:c0 + 3], in1=mhalf[:], op=ALU.pow),
                     reads=[f"rstd{c0}", "mhalf"], writes=[f"rstd{c0}"])

            def ln_a2(src_res, src, dst_res, dst, col):
                c0 = 12 + 4 * (col % 8)
                S.op("dve", lambda: nc.vector.scalar_tensor_tensor(out=stat[:, c0 + 3:c0 + 4], in0=stat[:, c0:c0 + 1], scalar=-1.0,
                                                                    in1=stat[:, c0 + 2:c0 + 3], op0=ALU.mult, op1=ALU.mult),
                     reads=[f"mv{c0}", f"rstd{c0}"], writes=[f"nmr{c0}"])
                S.op("act", lambda: nc.scalar.activation(out=dst, in_=src, func=AF.Identity,
                                                         scale=stat[:, c0 + 2:c0 + 3], bias=stat[:, c0 + 3:c0 + 4]),
                     reads=[src_res, f"rstd{c0}", f"nmr{c0}"], writes=[dst_res])

            def ln_b(dst_res, dst):
                S.op("dve", lambda: nc.vector.tensor_tensor(out=dst, in0=dst, in1=lnp[0][:], op=ALU.mult),
                     reads=[dst_res, "lnp"], writes=[dst_res])
                S.op("dve", lambda: nc.vector.tensor_tensor(out=dst, in0=dst, in1=lnp[1][:], op=ALU.add),
                     reads=[dst_res, "lnp"], writes=[dst_res])

            def ph_O1(wb):
                hold["wo0"] = wb[:, 0:4096].rearrange("p (k n) -> p k n", k=8)
                S.dma("sp", "lnp", [(lnp[0][:], lnp_d[0]), (lnp[1][:], lnp_d[1])], writes=["lnp"])
            add([(0, 8, 512, w_o_v[:, :, 0:512])], ph_O1)

            def ph_O2(wb):
                wo = [hold["wo0"], wb[:, 0:4096].rearrange("p (k n) -> p k n", k=8)]

                def tr(t):
                    sl = t % 2
                    tt = slice(t * 128, (t + 1) * 128)
                    for hf in range(2):
                        bk = 4 + 2 * sl + hf

                        def f(hf=hf, bk=bk, t=t):
                            for k4 in range(4):
                                kc = hf * 4 + k4
                                ins = nc.tensor.transpose(P[bk][:, k4 * 128:(k4 + 1) * 128], hh[:, t, kc * 128:(kc + 1) * 128], idf[:])
                            return ins
                        S.op("pe", f, reads=[f"hh{t}", "idf"], writes=[f"P{bk}"])
                        for k4 in range(4):
                            kc = hf * 4 + k4
                            if hf == 0:
                                S.op("act", lambda kc=kc, k4=k4, bk=bk, tt=tt: nc.scalar.activation(
                                    out=hT[:, kc, tt], in_=P[bk][:, k4 * 128:(k4 + 1) * 128], func=AF.Identity,
                                    scale=wcs[:, 3, kc:kc + 1], bias=wcs[:, 4, kc:kc + 1]),
                                    reads=[f"P{bk}", "wcs"], writes=[f"hT{t}"])
                            else:
                                S.op("dve", lambda kc=kc, k4=k4, bk=bk, tt=tt: nc.vector.tensor_scalar(
                                    out=hT[:, kc, tt], in0=P[bk][:, k4 * 128:(k4 + 1) * 128],
                                    scalar1=wcs[:, 3, kc:kc + 1], scalar2=wcs[:, 4, kc:kc + 1], op0=ALU.mult, op1=ALU.add),
                                    reads=[f"P{bk}", "wcs"], writes=[f"hT{t}"])

                for t, tl in enumerate(tiles):
                    sl = t % 2
                    tt = slice(t * 128, (t + 1) * 128)
                    S.dma("sp", f"xs{sl}", [(xsb[sl][:], tl["x"])], writes=[f"xs{sl}"])
                    if t > 0:
                        ln_a2(f"hh{t - 1}", hh[:, t - 1, :], f"hh{t - 1}", hh[:, t - 1, :], t - 1)
                    for nh in range(2):
                        bk = 2 * sl + nh

                        def f(nh=nh, bk=bk, tt=tt):
                            for kc in range(8):
                                ins = nc.tensor.matmul(P[bk][:, :], mT[:, kc, tt], wo[nh][:, kc, :], start=(kc == 0), stop=(kc == 7))
                            return ins
                        S.op("pe", f, reads=["W", "mT"], writes=[f"P{bk}"])
                        S.op("dve", lambda nh=nh, bk=bk, t=t, sl=sl: nc.vector.scalar_tensor_tensor(
                            out=hh[:, t, nh * 512:(nh + 1) * 512], in0=xsb[sl][:, nh * 512:(nh + 1) * 512], scalar=ALPHA,
                            in1=P[bk][:, :], op0=ALU.mult, op1=ALU.add),
                            reads=[f"P{bk}", f"xs{sl}"], writes=[f"hh{t}"])
                    ln_a1(f"hh{t}", hh[:, t, :], t)
                    if t > 1:
                        tr(t - 2)
                ln_a2(f"hh{nt - 1}", hh[:, nt - 1, :], f"hh{nt - 1}", hh[:, nt - 1, :], nt - 1)
                tr(max(0, nt - 2))
                deferred.append(lambda: tr(nt - 1))
            add([(0, 8, 512, w_o_v[:, :, 512:1024])], ph_O2)

            for jp in range(NJ // 2):
                def ph_F1(wb, jp=jp):
                    it = 0
                    for ji in range(2):
                        j = 2 * jp + ji
                        for (tok0, ntok, sg) in chunks:
                            tk = slice(tok0, tok0 + ntok)
                            b0 = 2 * (it % 4)
                            it += 1
                            for ab in range(2):
                                def f(ab=ab, tk=tk, b0=b0, ji=ji, ntok=ntok):
                                    for kc in range(8):
                                        ins = nc.tensor.matmul(P[b0 + ab][:, 0:ntok], wblk(wb, ab)[:, kc, ji * 128: ji * 128 + 128],
                                                               hT[:, kc, tk], start=(kc == 0), stop=(kc == 7))
                                    return ins
                                S.op("pe", f, reads=["W"] + [f"hT{t_}" for t_ in range(tok0 // 128, (tok0 + ntok) // 128)], writes=[f"P{b0 + ab}"])
                            if jp == 0 and it == 1:
                                run_deferred()
                            si = (b0 // 2) % 4
                            sa = scr[si][:, 0:ntok]
                            S.op("act", lambda: nc.scalar.activation(out=sa, in_=P[b0][:, 0:ntok], func=AF.Silu),
                                 reads=[f"P{b0}"], writes=[f"scr{si}"])
                            S.op("dve", lambda: nc.vector.tensor_tensor(out=zT[:, j, tk], in0=P[b0 + 1][:, 0:ntok], in1=sa, op=ALU.mult),
                                 reads=[f"P{b0 + 1}", f"scr{si}"], writes=["zT"])
                add([(0, 8, 256, w_up_v[:, :, jp * 256:(jp + 1) * 256]),
                     (2048, 8, 256, w_up_v[:, :, D_FF + jp * 256: D_FF + (jp + 1) * 256])], ph_F1)

            def f2_mm(w, t, bk):
                tt = slice(t * 128, (t + 1) * 128)

                def f():
                    for j in range(NJ):
                        ins = nc.tensor.matmul(P[bk][:, 0:256], zT[:, j, tt], w[:, j, :], start=(j == 0), stop=(j == NJ - 1))
                    return ins
                S.op("pe", f, reads=["W", "zT"], writes=[f"P{bk}"])

            for q in range(2):
                def ph_F2a(wb, q=q):
                    w = wb[:, 0:5632].rearrange("p (k n) -> p k n", k=NJ)
                    for t in range(nt):
                        bk = t % 8
                        f2_mm(w, t, bk)
                        for qq, with_psum in ((q, True), (q + 2, False)):
                            hs = hh[:, t, qq * 256:(qq + 1) * 256]
                            qs = slice(qq * 256, (qq + 1) * 256)
                            si = (2 * t + (0 if with_psum else 1)) % 4
                            tmp = scr[si][:, 0:256]
                            S.op("dve", lambda hs=hs, tmp=tmp, qs=qs: nc.vector.scalar_tensor_tensor(
                                out=tmp, in0=hs, scalar=ALPHA, in1=lnp[0][:, qs], op0=ALU.mult, op1=ALU.mult),
                                reads=[f"hh{t}", "lnp"], writes=[f"scr{si}"])
                            if with_psum:
                                S.op("dve", lambda tmp=tmp, qs=qs: nc.vector.scalar_tensor_tensor(
                                    out=tmp, in0=lnp[1][:, qs], scalar=ALPHA, in1=tmp, op0=ALU.mult, op1=ALU.add),
                                    reads=[f"scr{si}", "lnp"], writes=[f"scr{si}"])
                                S.op("dve", lambda hs=hs, tmp=tmp, bk=bk: nc.vector.tensor_tensor(out=hs, in0=tmp, in1=P[bk][:, 0:256], op=ALU.add),
                                     reads=[f"P{bk}", f"scr{si}"], writes=[f"hh{t}"])
                            else:
                                S.op("dve", lambda hs=hs, tmp=tmp, qs=qs: nc.vector.scalar_tensor_tensor(
                                    out=hs, in0=lnp[1][:, qs], scalar=ALPHA, in1=tmp, op0=ALU.mult, op1=ALU.add),
                                    reads=[f"scr{si}", "lnp"], writes=[f"hh{t}"])
                add([(0, NJ, 256, w_dn_v[:, :, q * 256:(q + 1) * 256])], ph_F2a)

            def ph_F2b(wb):
                hold["wd2"] = wb[:, 0:5632].rearrange("p (k n) -> p k n", k=NJ)
            add([(0, NJ, 256, w_dn_v[:, :, 512:768])], ph_F2b)

            def ph_F2c(wb):
                ws = [hold["wd2"], wb[:, 0:5632].rearrange("p (k n) -> p k n", k=NJ)]
                S.dma("sp", "lnp", [(lnp[0][:], lnp_d[2]), (lnp[1][:], lnp_d[3])], writes=["lnp"])

                R3f = R3[:].bitcast(F32)

                def stg(t):
                    p_ = t % 2
                    return R3f[:, p_ * 1024:(p_ + 1) * 1024], [f"stg{p_}"]

                def fin(t):
                    dst, names = stg(t)
                    S.op("dve", lambda: nc.vector.tensor_tensor(out=dst, in0=dst, in1=lnp[0][:], op=ALU.mult),
                         reads=names + ["lnp"], writes=names)
                    S.op("dve", lambda: nc.vector.tensor_tensor(out=dst, in0=dst, in1=lnp[1][:], op=ALU.add),
                         reads=names + ["lnp"], writes=names)
                    S.dma("act", f"yo{t % 2}", [(tiles[t]["y"], dst)], reads=names + [f"hT{t_}" for t_ in range(nt)] + ["cvT"])
                for t in range(nt):
                    sl = t % 2
                    for i_, qq in enumerate((2, 3)):
                        bk = (2 * t + i_) % 8
                        f2_mm(ws[i_], t, bk)
                        hs = hh[:, t, qq * 256:(qq + 1) * 256]
                        S.op("dve", lambda hs=hs, bk=bk: nc.vector.tensor_tensor(out=hs, in0=hs, in1=P[bk][:, 0:256], op=ALU.add),
                             reads=[f"P{bk}", f"hh{t}"], writes=[f"hh{t}"])
                    ln_a1(f"hh{t}", hh[:, t, :], t)
                    if t > 0:
                        fin(t - 1)
                    dst, names = stg(t)
                    c0 = 12 + 4 * (t % 8)
                    S.op("dve", lambda: nc.vector.scalar_tensor_tensor(out=stat[:, c0 + 3:c0 + 4], in0=stat[:, c0:c0 + 1], scalar=-1.0,
                                                                        in1=stat[:, c0 + 2:c0 + 3], op0=ALU.mult, op1=ALU.mult),
                         reads=[f"mv{c0}", f"rstd{c0}"], writes=[f"nmr{c0}"])
                    S.op("act", lambda: nc.scalar.activation(out=dst, in_=hh[:, t, :], func=AF.Identity,
                                                             scale=stat[:, c0 + 2:c0 + 3], bias=stat[:, c0 + 3:c0 + 4]),
                         reads=[f"hh{t}", f"rstd{c0}", f"nmr{c0}"], writes=names)
                fin(nt - 1)
            add([(0, NJ, 256, w_dn_v[:, :, 768:1024])], ph_F2c)

        def ptile(r0):
            return dict(kind=0, x=xp[r0:r0 + 128, :], y=yp[r0:r0 + 128, :])

        def stile(i):
            return dict(kind=1, x=xsm[i * 128:(i + 1) * 128, :], y=ys[i * 128:(i + 1) * 128, :], seqs=(2 * i, 2 * i + 1))
        NS = NSAMP * DEC_SEQ
        emit_st([stile(0), stile(1)] + [ptile(128 * i) for i in range(4)],
                [(0, 256, 0), (256, 512, 1)],
                [dict(kind=1, tok0=0, L=DEC_SEQ, nseq=NSAMP, ubase=0), dict(kind=0, tok0=256, L=512, nseq=1, ubase=NSAMP * (DEC_SEQ + 2))],
                0, True, False)
        emit_st([ptile(512 + 128 * i) for i in range(6)], [(0, 384, 0), (384, 384, 0)],
                [dict(kind=0, tok0=0, L=768, nseq=1, ubase=0)], NS + 512, False, False)
        emit_st([ptile(1280 + 128 * i) for i in range(6)], [(0, 384, 0), (384, 384, 0)],
                [dict(kind=0, tok0=0, L=768, nseq=1, ubase=0)], NS + 1280, False, True)

        widx = [i for i, (l, _) in enumerate(steps) if l is not None]
        slot_of = {si: k % NWB for k, si in enumerate(widx)}
        emitted = 0

        def emit_load(k):
            si = widx[k]
            sl = slot_of[si]
            pairs = []
            for (off, kk, ncol, src) in steps[si][0]:
                dst = Wb[sl][:, off:off + kk * ncol].rearrange("p (k n) -> p k n", k=kk)
                pairs.append((dst, src))
            S.dma("pool", f"W{sl}", pairs, writes=[f"Wb{sl}"])

        class _Alias:
            cur = []
        orig_sync, orig_record = S._sync, S._record

        def _map(names):
            out = []
            for n in names:
                if n == "W":
                    out.extend(_Alias.cur)
                else:
                    out.append(n)
            return out
        S._sync = lambda eng, r, w: orig_sync(eng, _map(r), _map(w))
        S._record = lambda tok, r, w: orig_record(tok, _map(r), _map(w))

        prev_w = None
        for si, (loads, fn) in enumerate(steps):
            if loads is not None:
                k = widx.index(si)
                while emitted < min(len(widx), k + NWB - 1):
                    emit_load(emitted)
                    emitted += 1
                sl = slot_of[si]
                _Alias.cur = [f"Wb{sl}"] + ([f"Wb{prev_w}"] if prev_w is not None else [])
                fn(Wb[sl])
                prev_w = sl
            else:
                _Alias.cur = []
                fn(None)
            if stop_after is not None and si == stop_after:
                S.barrier()
                d1 = nc.dram_tensor("dbg_R1", [128, R1N], BF16, kind="ExternalOutput").ap()
                d2 = nc.dram_tensor("dbg_R2", [128, NTL * 1024], F32, kind="ExternalOutput").ap()
                d3 = nc.dram_tensor("dbg_R3", [128, 8 * TS], BF16, kind="ExternalOutput").ap()
                S.dma("sp", "dbg", [(d1[:, :], R1[:]), (d2[:, :], R2[:]), (d3[:, :], R3[:])])
                break
        S.finish()
    return nc


_CACHE = {}


def kernel(x_prompt, x_sample, state_ret, cache_conv, w_in, w_conv, w_ret_out, w_conv_out,
           w_o, ln1_g, ln1_b, w_up, w_down, ln2_g, ln2_b):
    f32 = np.float32
    consts, cdec = _constants()
    if "nc" not in _CACHE:
        _CACHE["nc"] = build_program(cdec)
    nc = _CACHE["nc"]

    A = lambda a: np.ascontiguousarray(np.asarray(a, dtype=f32))
    x_prompt, x_sample = A(x_prompt), A(x_sample)
    state_ret, cache_conv = A(state_ret), A(cache_conv)
    wc = A(np.concatenate([np.asarray(w_conv)[0].reshape(3, 8, 128).transpose(2, 0, 1).reshape(128, 24),
                           np.asarray(ln1_g)[0].reshape(8, 128).T, np.asarray(ln1_b)[0].reshape(8, 128).T], axis=1))
    lnp = A(np.stack([np.broadcast_to(np.asarray(v)[0][None, :], (128, D)) for v in (ln1_g, ln1_b, ln2_g, ln2_b)]))
    shared = dict(w_in=A(w_in)[0], w_ro=A(w_ret_out)[0], w_co=A(w_conv_out)[0], w_o=A(w_o)[0],
                  w_up=A(w_up)[0], w_dn=A(w_down)[0], wc=wc, lnp=lnp, **consts)
    in_maps = []
    for c in range(NCORES):
        m = dict(shared)
        m["xp"] = x_prompt[c]
        m["xs"] = x_sample[NSAMP * c:NSAMP * (c + 1)].reshape(NSAMP * DEC_SEQ, D)
        m["sret"] = state_ret[0, NSAMP * c:NSAMP * (c + 1)]
        m["cconv"] = cache_conv[0, NSAMP * c:NSAMP * (c + 1)]
        in_maps.append(m)
    res = run_bass_kernel_spmd(nc, in_maps, core_ids=list(range(NCORES)))
    R = res.results
    y_p = np.stack([R[c]["yp"] for c in range(NCORES)]).astype(f32)
    y_s = np.concatenate([R[c]["ys"].reshape(NSAMP, DEC_SEQ, D) for c in range(NCORES)]).astype(f32)
    s_p = np.stack([R[c]["sp_o"] for c in range(NCORES)])[None].astype(f32)
    c_p = np.stack([R[c]["cp_o"] for c in range(NCORES)])[None].astype(f32)
    s_s = np.concatenate([R[c]["ss_o"] for c in range(NCORES)])[None].astype(f32)
    c_s = np.concatenate([R[c]["cs_o"] for c in range(NCORES)])[None].astype(f32)
    return (y_p, y_s, s_p, c_p, s_s, c_s)
```
